# Optimizing a Trainium2 kernel written in Bass

```python
import math
import jax, jax.numpy as jnp
from jax import lax
import numpy as np

D_MODEL = 1024
BATCH = 8
SEQ = 8192
DEPTH = 2
DEC_BATCH = 32
DEC_SEQ = 64
PAST_LEN = 1024

CHUNK = 64
D_MIX = D_MODEL
D_A = D_MIX // 4
D_B = D_MIX // 4
D_C = D_MIX - D_A - D_B
GDN_HEADS = 4
GDN_DK = D_C // GDN_HEADS
GDN_DV = D_C // GDN_HEADS
CONV_A = 3
CONV_B = 31
CONV_QKV = 4
D_FF = -(-(8 * D_MODEL) // (3 * 256)) * 256
D_IN = 3 * D_A + 2 * D_B + 4 * D_C + 2 * GDN_HEADS
EPS = 1e-6

kernel_name = 'hybrid_conv_deltanet_stream_step'


def _rmsnorm(x, g):
    x32 = x.astype(jnp.float32)
    y = x32 * lax.rsqrt(jnp.mean(x32 * x32, axis=-1, keepdims=True) + EPS)
    return (y * g.astype(jnp.float32)).astype(x.dtype)


def _layernorm(x, g, b):
    x32 = x.astype(jnp.float32)
    mu = jnp.mean(x32, axis=-1, keepdims=True)
    xc = x32 - mu
    y = xc * lax.rsqrt(jnp.mean(xc * xc, axis=-1, keepdims=True) + EPS)
    return (y * g.astype(jnp.float32) + b.astype(jnp.float32)).astype(x.dtype)


def _l2norm(t):
    return t * lax.rsqrt(jnp.sum(t * t, axis=-1, keepdims=True) + EPS)


def _causal_dwconv(u, buf, w):
    width = w.shape[0]
    padded = jnp.concatenate([buf.astype(u.dtype), u], axis=1)
    y = lax.conv_general_dilated(padded, w[:, None, :].astype(u.dtype), window_strides=(1,),
                                 padding='VALID', dimension_numbers=('NWC', 'WIO', 'NWC'),
                                 feature_group_count=u.shape[-1])
    return y, padded[:, -(width - 1):, :]


def _gdn_block(q, k, v, g, beta, S):
    L = q.shape[2]
    incl = jnp.tril(jnp.ones((L, L), dtype=bool))
    strict = jnp.tril(jnp.ones((L, L), dtype=bool), -1)
    gam = jnp.cumsum(g, axis=-1)
    decay = jnp.exp(jnp.where(incl, gam[..., :, None] - gam[..., None, :], -jnp.inf))
    kk = jnp.einsum('bhid,bhjd->bhij', k, k)
    a_low = jnp.where(strict, beta[..., :, None] * kk * decay, 0.0)
    eye = jnp.eye(L, dtype=q.dtype)
    rhs = jnp.concatenate([v * beta[..., None], k * (beta * jnp.exp(gam))[..., None]], axis=-1)
    sol = lax.linalg.triangular_solve(a_low + eye, rhs, left_side=True, lower=True)
    dv = v.shape[-1]
    u, w = sol[..., :dv], sol[..., dv:]
    v_new = u - jnp.einsum('bhlk,bhkv->bhlv', w, S)
    qk = jnp.einsum('bhid,bhjd->bhij', q, k) * decay
    o = (jnp.einsum('bhlk,bhkv->bhlv', q * jnp.exp(gam)[..., None], S)
         + jnp.einsum('bhij,bhjv->bhiv', qk, v_new))
    g_last = gam[..., -1:]
    S_new = (S * jnp.exp(g_last)[..., None]
             + jnp.einsum('bhlk,bhlv->bhkv', k * jnp.exp(g_last - gam)[..., None], v_new))
    return o, S_new


def _gdn(q, k, v, g, beta, S0):
    Bsz, H, T, _ = q.shape
    L = min(T, CHUNK)
    N = T // L

    def blocks(t):
        return jnp.moveaxis(t.reshape(t.shape[:2] + (N, L) + t.shape[3:]), 2, 0)

    def step(S, inp):
        o, S = _gdn_block(*inp, S)
        return S, o

    S, o = lax.scan(step, S0, (blocks(q), blocks(k), blocks(v), blocks(g), blocks(beta)))
    o = jnp.moveaxis(o, 0, 2).reshape(Bsz, H, T, -1)
    return o, S


def _mixer(h, buf_a, buf_b, buf_qkv, S0, p):
    Bsz, T, _ = h.shape
    f32 = jnp.float32
    z = h @ p['w_in']
    sizes = (D_A,) * 3 + (D_B,) * 2 + (D_C,) * 4 + (GDN_HEADS,) * 2
    cuts = np.cumsum(sizes)[:-1].tolist()
    h_a, b_a, c_a, p_a, p_g, q, k, v, z_g, a_dec, b_beta = jnp.split(z, cuts, axis=-1)
    conv_a, nbuf_a = _causal_dwconv(c_a * h_a, buf_a, p['conv_a_w'])
    y_a = b_a * conv_a
    glu = p_a * jax.nn.sigmoid(p_g)
    conv_b, nbuf_b = _causal_dwconv(glu, buf_b, p['conv_b_w'])
    y_b = jax.nn.silu(_layernorm(conv_b + p['conv_b_b'], p['ln_b_g'], p['ln_b_b']))
    qkv, nbuf_qkv = _causal_dwconv(jnp.concatenate([q, k, v], axis=-1), buf_qkv, p['conv_qkv_w'])
    qkv = jax.nn.silu(qkv).astype(f32)
    q, k, v = jnp.split(qkv, 3, axis=-1)
    heads = lambda t: t.reshape(Bsz, T, GDN_HEADS, -1).transpose(0, 2, 1, 3)
    q = _l2norm(heads(q)) * (GDN_DK ** -0.5)
    k = _l2norm(heads(k))
    v = heads(v)
    beta = jax.nn.sigmoid(b_beta.astype(f32)).transpose(0, 2, 1)
    g = (-jnp.exp(p['a_log'].astype(f32))
         * jax.nn.softplus(a_dec.astype(f32) + p['dt_bias'].astype(f32))).transpose(0, 2, 1)
    o, S = _gdn(q, k, v, g, beta, S0.astype(f32))
    o = o.transpose(0, 2, 1, 3)
    o = _rmsnorm(o, p['gdn_norm_g']) * jax.nn.silu(z_g.astype(f32).reshape(Bsz, T, GDN_HEADS, GDN_DV))
    y_c = o.reshape(Bsz, T, D_C).astype(h.dtype)
    y = jnp.concatenate([y_a, y_b, y_c], axis=-1) @ p['w_out']
    return y, nbuf_a, nbuf_b, nbuf_qkv, S.astype(h.dtype)


def _trunk(x, c, st_a, st_b, st_qkv, st_s, layers, final_norm_g):
    cmod = jax.nn.silu(c)
    new_a, new_b, new_q, new_s = [], [], [], []
    for l in range(DEPTH):
        p = {name: arr[l] for name, arr in layers.items()}
        mod = cmod @ p['w_ada'] + p['b_ada']
        sh1, sc1, g1, sh2, sc2, g2 = jnp.split(mod[:, None, :], 6, axis=-1)
        h = _rmsnorm(x, p['norm1_g']) * (1 + sc1) + sh1
        y, na, nb, nq, ns = _mixer(h, st_a[l], st_b[l], st_qkv[l], st_s[l], p)
        x = x + g1 * y
        h = _rmsnorm(x, p['norm2_g']) * (1 + sc2) + sh2
        gate, up = jnp.split(h @ p['w_gate_up'], 2, axis=-1)
        x = x + g2 * ((jax.nn.silu(gate) * up) @ p['w_down'])
        new_a.append(na)
        new_b.append(nb)
        new_q.append(nq)
        new_s.append(ns)
    return (_rmsnorm(x, final_norm_g), jnp.stack(new_a), jnp.stack(new_b),
            jnp.stack(new_q), jnp.stack(new_s))


def setup_inputs(seed: int = 0) -> dict:
    key = jax.random.key(seed)
    ks = jax.random.split(key, 32)
    f32 = jnp.float32

    def nrm(i, shape, scale=1.0):
        return scale * jax.random.normal(ks[i], shape, f32)

    dt = jnp.exp(jax.random.uniform(ks[19], (DEPTH, GDN_HEADS), f32, math.log(1e-3), math.log(1e-1)))
    return {
        'x_prompt': nrm(0, (BATCH, SEQ, D_MODEL)),
        'x_sample': nrm(1, (DEC_BATCH, DEC_SEQ, D_MODEL)),
        'state_conv_a': nrm(2, (DEPTH, DEC_BATCH, CONV_A - 1, D_A)),
        'state_conv_b': nrm(3, (DEPTH, DEC_BATCH, CONV_B - 1, D_B)),
        'state_conv_qkv': nrm(4, (DEPTH, DEC_BATCH, CONV_QKV - 1, 3 * D_C)),
        'state_gdn': nrm(5, (DEPTH, DEC_BATCH, GDN_HEADS, GDN_DK, GDN_DV), 0.1),
        'c_prompt': nrm(6, (BATCH, D_MODEL)),
        'c_sample': nrm(7, (DEC_BATCH, D_MODEL)),
        'norm1_g': 1.0 + nrm(8, (DEPTH, D_MODEL), 0.01),
        'w_ada': nrm(9, (DEPTH, D_MODEL, 6 * D_MODEL), 0.5 * D_MODEL ** -0.5),
        'b_ada': nrm(10, (DEPTH, 6 * D_MODEL), 0.01),
        'w_in': nrm(11, (DEPTH, D_MODEL, D_IN), D_MODEL ** -0.5),
        'conv_a_w': nrm(12, (DEPTH, CONV_A, D_A), CONV_A ** -0.5),
        'conv_b_w': nrm(13, (DEPTH, CONV_B, D_B), CONV_B ** -0.5),
        'conv_b_b': nrm(14, (DEPTH, D_B), 0.01),
        'ln_b_g': 1.0 + nrm(15, (DEPTH, D_B), 0.01),
        'ln_b_b': nrm(16, (DEPTH, D_B), 0.01),
        'conv_qkv_w': nrm(17, (DEPTH, CONV_QKV, 3 * D_C), CONV_QKV ** -0.5),
        'a_log': jnp.log(jax.random.uniform(ks[18], (DEPTH, GDN_HEADS), f32, 1.0, 16.0)),
        'dt_bias': dt + jnp.log(-jnp.expm1(-dt)),
        'gdn_norm_g': 1.0 + nrm(20, (DEPTH, GDN_DV), 0.01),
        'w_out': nrm(21, (DEPTH, D_MIX, D_MODEL), D_MIX ** -0.5),
        'norm2_g': 1.0 + nrm(22, (DEPTH, D_MODEL), 0.01),
        'w_gate_up': nrm(23, (DEPTH, D_MODEL, 2 * D_FF), D_MODEL ** -0.5),
        'w_down': nrm(24, (DEPTH, D_FF, D_MODEL), D_FF ** -0.5),
        'final_norm_g': 1.0 + nrm(25, (D_MODEL,), 0.01),
    }


def reference(x_prompt, x_sample, state_conv_a, state_conv_b, state_conv_qkv, state_gdn,
              c_prompt, c_sample, norm1_g, w_ada, b_ada, w_in, conv_a_w, conv_b_w, conv_b_b,
              ln_b_g, ln_b_b, conv_qkv_w, a_log, dt_bias, gdn_norm_g, w_out, norm2_g,
              w_gate_up, w_down, final_norm_g):
    layers = {
        'norm1_g': norm1_g, 'w_ada': w_ada, 'b_ada': b_ada, 'w_in': w_in,
        'conv_a_w': conv_a_w, 'conv_b_w': conv_b_w, 'conv_b_b': conv_b_b,
        'ln_b_g': ln_b_g, 'ln_b_b': ln_b_b, 'conv_qkv_w': conv_qkv_w,
        'a_log': a_log, 'dt_bias': dt_bias, 'gdn_norm_g': gdn_norm_g, 'w_out': w_out,
        'norm2_g': norm2_g, 'w_gate_up': w_gate_up, 'w_down': w_down,
    }
    bp = x_prompt.shape[0]
    dtp = x_prompt.dtype
    y_prompt, pa, pb, pq, ps = _trunk(
        x_prompt, c_prompt,
        jnp.zeros((DEPTH, bp, CONV_A - 1, D_A), dtp),
        jnp.zeros((DEPTH, bp, CONV_B - 1, D_B), dtp),
        jnp.zeros((DEPTH, bp, CONV_QKV - 1, 3 * D_C), dtp),
        jnp.zeros((DEPTH, bp, GDN_HEADS, GDN_DK, GDN_DV), dtp),
        layers, final_norm_g)
    y_sample, sa, sb, sq, ss = _trunk(
        x_sample, c_sample, state_conv_a, state_conv_b, state_conv_qkv, state_gdn,
        layers, final_norm_g)
    return (y_prompt, y_sample, pa, pb, pq, ps, sa, sb, sq, ss)
```

```python
import numpy as np
from contextlib import ExitStack
import concourse.bass as bass
import concourse.mybir as mybir
from concourse.bass_utils import run_bass_kernel_spmd

F32 = mybir.dt.float32
BF16 = mybir.dt.bfloat16
ALU = mybir.AluOpType
AF = mybir.ActivationFunctionType

D = 1024
DEPTH = 2
TP = 8192
NS = 4
TS = 64
D_A = 256
D_B = 256
D_C = 512
H = 4
DK = 128
D_FF = 2816
D_IN = 3336
EPS = 1e-6
NSTREAM = 1 + NS
KC = D // 128
FC = D_FF // 128
NCV = 116
CV_A, CV_B, CV_Q = 0, 6, 68


class Trk:
    __slots__ = ("w", "r", "excl")

    def __init__(self, excl=False):
        self.w = None
        self.r = {}
        self.excl = excl


def _split(reads, writes):
    xr = [r for r in reads if r.excl]
    if not xr:
        return reads, writes
    return [r for r in reads if not r.excl], list(writes) + xr


class Buf:
    def __init__(self, t):
        self.t = t
        self._k = {}

    def k(self, key=0):
        tr = self._k.get(key)
        if tr is None:
            tr = self._k[key] = Trk()
        return tr

    def ks(self, keys):
        return [self.k(x) for x in keys]


class BankBuf(Buf):
    def k(self, key=0):
        tr = self._k.get(0)
        if tr is None:
            tr = self._k[0] = Trk(excl=True)
        return tr


class Sched:
    ENG = ("pe", "dve", "act", "pool", "sp")

    def __init__(self, nc, es):
        self.nc = nc
        self.e = {"pe": nc.tensor, "dve": nc.vector, "act": nc.scalar, "pool": nc.gpsimd, "sp": nc.sync}
        self.sem = {k: es.enter_context(nc.semaphore("sem_" + k)) for k in ("pe", "dve", "act", "pool")}
        self.cnt = {k: 0 for k in self.sem}
        self.waited = {k: {} for k in self.ENG}
        self.ndma = 24
        self.dsem = {q: [es.enter_context(nc.semaphore("dsem_%s%d" % (q, i))) for i in range(self.ndma)]
                     for q in ("sp", "pool")}
        self.dcnt = {q: [0] * self.ndma for q in ("sp", "pool")}
        self.dnext = {q: 0 for q in ("sp", "pool")}
        self.n_ins = 0

    def _wait(self, eng, deps):
        w = self.waited[eng]
        best = {}
        for d in deps:
            if d is None:
                continue
            sem, val = d
            if val > best.get(id(sem), (None, 0))[1]:
                best[id(sem)] = (sem, val)
        for sem, val in best.values():
            if w.get(id(sem), 0) < val:
                self.e[eng].wait_ge(sem, val)
                w[id(sem)] = val

    def _deps(self, eng, reads, writes):
        deps = []
        mysem = self.sem.get(eng)
        for r in reads:
            deps.append(r.w)
        for t in writes:
            skip_own = (eng == "pe" and t.excl)
            if t.w is not None and not (skip_own and t.w[0] is mysem):
                deps.append(t.w)
            for d in t.r.values():
                if not (skip_own and d[0] is mysem):
                    deps.append(d)
        return deps

    def op(self, eng, fn, reads=(), writes=()):
        reads, writes = _split(reads, writes)
        self._wait(eng, self._deps(eng, reads, writes))
        ins = fn(self.e[eng])
        self.cnt[eng] += 1
        ins.then_inc(self.sem[eng], 1)
        me = (self.sem[eng], self.cnt[eng])
        for r in reads:
            r.r[eng] = me
        for t in writes:
            t.w = me
            t.r = {}
        self.n_ins += 1
        return me

    def mm(self, out_ap, pairs, reads, writes, transpose=False):
        reads, writes = _split(reads, writes)
        self._wait("pe", self._deps("pe", reads, writes))
        pe = self.e["pe"]
        n = len(pairs)
        ins = None
        for i, (l, r) in enumerate(pairs):
            ins = pe.matmul(out_ap, lhsT=l, rhs=r, start=(i == 0), stop=(i == n - 1))
        self.cnt["pe"] += 1
        ins.then_inc(self.sem["pe"], 1)
        me = (self.sem["pe"], self.cnt["pe"])
        for r in reads:
            r.r["pe"] = me
        for t in writes:
            t.w = me
            t.r = {}
        self.n_ins += n
        return me

    def mm_gen(self, out_ap, pairs, reads, writes, chunk=4):
        reads, writes = _split(reads, writes)
        self._wait("pe", self._deps("pe", reads, writes))
        pe = self.e["pe"]
        n = len(pairs)
        ins = None
        for i, (l, r) in enumerate(pairs):
            ins = pe.matmul(out_ap, lhsT=l, rhs=r, start=(i == 0), stop=(i == n - 1))
            if (i + 1) % chunk == 0 and i != n - 1:
                yield
        self.cnt["pe"] += 1
        ins.then_inc(self.sem["pe"], 1)
        me = (self.sem["pe"], self.cnt["pe"])
        for r in reads:
            r.r["pe"] = me
        for t in writes:
            t.w = me
            t.r = {}
        self.n_ins += n

    def dma(self, out_ap, in_ap, reads=(), writes=(), eng="sp"):
        i = self.dnext[eng]
        self.dnext[eng] = (i + 1) % self.ndma
        sem = self.dsem[eng][i]
        cnt = self.dcnt[eng]
        deps = self._deps(eng, reads, writes)
        if cnt[i] > 0:
            deps.append((sem, cnt[i]))
        self._wait(eng, deps)
        cnt[i] += 16
        self.e[eng].dma_start(out=out_ap, in_=in_ap).then_inc(sem, 16)
        me = (sem, cnt[i])
        for r in reads:
            r.r["dma_%s%d" % (eng, i)] = me
        for t in writes:
            t.w = me
            t.r = {}
        self.n_ins += 1
        return me

    def finish(self):
        deps = [(self.dsem[q][i], self.dcnt[q][i]) for q in self.dsem for i in range(self.ndma) if self.dcnt[q][i] > 0]
        deps += [(self.sem[k], self.cnt[k]) for k in self.sem if self.cnt[k] > 0]
        self._wait("sp", deps)


class Ring:
    def __init__(self, items):
        self.items = items
        self.i = 0

    def next(self):
        it = self.items[self.i]
        self.i = (self.i + 1) % len(self.items)
        return it


class _Stop(Exception):
    pass


def build(tp=TP, ns=NS, dbg=False):
    nc = bass.Bass("TRN2", target_bir_lowering=False)
    es = ExitStack()
    with es:
        _build(nc, es, tp, ns, dbg)
    return nc


def _build(nc, es, tp, ns, dbg):
    S = Sched(nc, es)
    nstream = 1 + ns
    stop_at = dbg if isinstance(dbg, int) and not isinstance(dbg, bool) else None

    CKL = [0]

    def ck(n):
        if stop_at is not None and n + 100 * CKL[0] == stop_at:
            raise _Stop()

    def din(name, shape):
        return nc.dram_tensor(name, list(shape), F32, kind="ExternalInput").ap()

    def dout(name, shape):
        return nc.dram_tensor(name, list(shape), F32, kind="ExternalOutput").ap()

    xp = din("xp", [tp, D])
    xs = din("xs", [NS, TS, D])
    st_a = din("st_a", [DEPTH, NS, 2, D_A])
    st_b = din("st_b", [DEPTH, NS, 30, D_B])
    st_q = din("st_q", [DEPTH, NS, 3, 3 * D_C])
    st_s = din("st_s", [DEPTH, NS, H, DK, DK])
    c_all = din("c_all", [NSTREAM, D])
    norm1_g = din("norm1_g", [DEPTH, D])
    w_ada = din("w_ada", [DEPTH, D, 6 * D])
    b_ada = din("b_ada", [DEPTH, 6 * D])
    w_in = din("w_in", [DEPTH, D, D_IN])
    conv_a_w = din("conv_a_w", [DEPTH, 3, D_A])
    conv_b_w = din("conv_b_w", [DEPTH, 31, D_B])
    conv_b_b = din("conv_b_b", [DEPTH, D_B])
    ln_b_g = din("ln_b_g", [DEPTH, D_B])
    ln_b_b = din("ln_b_b", [DEPTH, D_B])
    conv_qkv_w = din("conv_qkv_w", [DEPTH, 4, 3 * D_C])
    a_log = din("a_log", [DEPTH, H])
    dt_bias = din("dt_bias", [DEPTH, H])
    gdn_norm_g = din("gdn_norm_g", [DEPTH, DK])
    w_out = din("w_out", [DEPTH, D, D])
    norm2_g = din("norm2_g", [DEPTH, D])
    w_gu = din("w_gate_up", [DEPTH, D, 2 * D_FF])
    w_dn = din("w_down", [DEPTH, D_FF, D])
    final_g = din("final_norm_g", [1, D])

    y_p = dout("y_p", [tp, D])
    y_s = dout("y_s", [NS, TS, D])
    na_p = dout("na_p", [DEPTH, 2, D_A])
    nb_p = dout("nb_p", [DEPTH, 30, D_B])
    nq_p = dout("nq_p", [DEPTH, 3, 3 * D_C])
    ns_p = dout("ns_p", [DEPTH, H, DK, DK])
    na_s = dout("na_s", [DEPTH, NS, 2, D_A])
    nb_s = dout("nb_s", [DEPTH, NS, 30, D_B])
    nq_s = dout("nq_s", [DEPTH, NS, 3, 3 * D_C])
    ns_s = dout("ns_s", [DEPTH, NS, H, DK, DK])

    def dscr(name, shape):
        return nc.dram_tensor(name, list(shape), BF16).ap()

    wb_ada = dscr("wb_ada", [DEPTH, D, 6 * D])
    wb_in = dscr("wb_in", [DEPTH, D, D_IN])
    wb_out = dscr("wb_out", [DEPTH, D, D])
    wb_gu = dscr("wb_gu", [DEPTH, D, 2 * D_FF])
    wb_dn = dscr("wb_dn", [DEPTH, D_FF, D])
    wb_cv = dscr("wb_cv", [DEPTH, 128, 128, 128])

    def sb(name, shape, dt=F32):
        return Buf(es.enter_context(nc.sbuf_tensor(name, list(shape), dt)))

    def ps(name, shape, dt=F32):
        return Buf(es.enter_context(nc.psum_tensor(name, list(shape), dt)))

    NW = 4
    wslot = [sb("wslot%d" % i, [128, 8, 512], BF16) for i in range(NW)]
    ident_f = sb("ident_f", [128, 128])
    ident_b = sb("ident_b", [128, 128], BF16)
    negmask = sb("negmask", [128, 128])
    onecol4 = sb("onecol4", [128, 4, 4], BF16)
    onehot4 = sb("onehot4", [4, 4, 128])
    ones_row = sb("ones_row", [1, 128])
    mean1024 = sb("mean1024", [128, 1], BF16)
    mean256 = sb("mean256", [128, 1], BF16)
    stage = sb("stage", [128, 128])

    pcols = sb("pcols", [128, 304])
    C_N1G, C_N2G, C_FNG, C_CBB, C_LNG, C_LNB, C_GNG, C_CAW, C_CBW, C_CQW = 0, 16, 32, 40, 44, 48, 52, 54, 66, 190
    C_LNGH, C_LNBH = 288, 292
    bada = sb("bada", [128, 96])
    hcols = sb("hcols", [4, 8])
    cmod_b = sb("cmod_b", [128, KC, 8], BF16)
    cfm = sb("cfm", [128, KC, 8])
    mcol = sb("mcol", [128, DEPTH, 6, KC, 8])
    halfmc = sb("halfmc", [128, 4])

    NMAX = 512
    xres = sb("xres", [128, KC, NMAX])
    xtok = sb("xtok", [128, 2, D])
    hbuf = sb("hbuf", [128, KC, NMAX], BF16)
    tmpf = [sb("tmpf%d" % i, [128, NMAX]) for i in range(3)]
    zA = sb("zA", [128, 6, NMAX])
    ua = sb("ua", [128, 2, 2 + NMAX], BF16)
    gb = sb("gb", [128, 2, 30 + NMAX], BF16)
    QBW = 3 + NMAX
    QKO = 13 * NMAX
    arena = sb("arena", [128, QKO + 12 * NMAX], BF16)

    class _V:
        pass
    qb = _V(); qb.t = arena.t[:, 0:12 * QBW].rearrange("p (c n) -> p c n", n=QBW); qb.k = arena.k
    qkvs = _V(); qkvs.t = arena.t[:, QKO:QKO + 12 * NMAX].rearrange("p (c n) -> p c n", n=NMAX); qkvs.k = arena.k
    ffa = _V(); ffa.t = arena.t[:, 0:FC * NMAX].rearrange("p (c n) -> p c n", n=NMAX); ffa.k = arena.k
    modt = _V(); modt.t = zA.t[:, 0:2, 0:384].rearrange("p l (c s) -> p l c s", s=8); modt.k = zA.k
    hist_a = sb("hist_a", [128, DEPTH, NS, 2, 2], BF16)
    hist_b = sb("hist_b", [128, DEPTH, NS, 2, 30], BF16)
    hist_q = sb("hist_q", [128, DEPTH, NS, 12, 3], BF16)
    tail_a = sb("tail_a", [128, DEPTH, NS, 2, 2])
    tail_b = sb("tail_b", [128, DEPTH, NS, 30, 2])
    tail_q = sb("tail_q", [128, DEPTH, NS, 3, 12])
    qn = sb("qn", [128, H, NMAX], BF16)
    qe = sb("qe", [128, H, NMAX], BF16)
    kn = sb("kn", [128, H, NMAX], BF16)
    zg = sb("zg", [128, H, NMAX], BF16)
    ymix = sb("ymix", [128, KC, NMAX], BF16)
    rows = {n: sb("r_" + n, [4, NMAX]) for n in
            ("t0", "g", "gam", "ngam", "c3", "egam", "c1", "c4", "rq", "rqe", "rk", "t1")}
    egl_r = sb("egl_r", [4, 4])
    egl_c = sb("egl_c", [128, H, 4])
    cols = [sb("cols%d" % i, [128, 16]) for i in range(4)]
    S_f = sb("S_f", [128, DEPTH, H, DK])
    S_b = sb("S_b", [128, DEPTH, H, DK], BF16)
    def g4(name, dt=BF16):
        return sb("g4_" + name, [128, H, 128], dt)
    G_D = g4("D", F32)
    G_Dn = g4("Dn", F32)
    G_X0, G_Y0 = g4("X0"), g4("Y0")
    G_X = [g4("Xa"), g4("Xb")]
    G_Y = [g4("Ya"), g4("Yb")]
    G_Q = [g4("Qa"), g4("Qb")]
    G_Xo = [g4("Xo%d" % j) for j in range(3)]
    G_T, G_V, G_P, G_PT = g4("T"), g4("V"), g4("P"), g4("PT")
    G_kw, G_kh, G_vb, G_nw, G_vn, G_on = g4("kw"), g4("kh"), g4("vb"), g4("nw"), g4("vn"), g4("on")
    G_sc = sb("g4_sc", [128, 16])
    bmask = [sb("bmask%d" % i, [128, 128], BF16) for i in range(4)]
    esel = sb("esel", [8, 128])
    smask = sb("smask", [128, 128], BF16)

    def psb(name, shape, dt=F32):
        return BankBuf(es.enter_context(nc.psum_tensor(name, list(shape), dt)))

    pm = [psb("pm%d" % i, [128, 512]) for i in range(3)]
    pr = [psb("pr0", [128, 512])]
    ph = [psb("ph%d" % i, [128, 4, 128]) for i in range(4)]
    phb = [ph[i].t[:].bitcast(BF16) for i in range(4)]
    hrings = [Ring(list(range(4))) for _ in range(4)]
    pm_ring = Ring(pm)
    pr_ring = Ring(pr)
    pg_ring = Ring([(ph[i], j) for j in range(4) for i in range(4)])
    tmpf_ring = Ring(tmpf)

    def ACT(out, in_, func, reads, writes, bias=0.0, scale=1.0, accum=None):
        if accum is None:
            return S.op("act", lambda e: e.activation(out=out, in_=in_, func=func, bias=bias, scale=scale), reads, writes)
        return S.op("act", lambda e: e.activation(out=out, in_=in_, func=func, bias=bias, scale=scale,
                                                  accum_out=accum), reads, writes)

    def TT(eng, out, a, b, op, reads, writes):
        return S.op(eng, lambda e: e.tensor_tensor(out=out, in0=a, in1=b, op=op), reads, writes)

    def TSC(eng, out, a, s1, s2, op0, op1, reads, writes):
        if s2 is None:
            return S.op(eng, lambda e: e.tensor_scalar(out=out, in0=a, scalar1=s1, scalar2=None, op0=op0), reads, writes)
        return S.op(eng, lambda e: e.tensor_scalar(out=out, in0=a, scalar1=s1, scalar2=s2, op0=op0, op1=op1), reads, writes)

    def STT(eng, out, in0, scalar, in1, op0, op1, reads, writes):
        return S.op(eng, lambda e: e.scalar_tensor_tensor(out=out, in0=in0, scalar=scalar, in1=in1, op0=op0, op1=op1),
                    reads, writes)

    def CP(eng, out, in_, reads, writes):
        if eng == "act":
            return S.op("act", lambda e: e.copy(out=out, in_=in_), reads, writes)
        return S.op(eng, lambda e: e.tensor_copy(out=out, in_=in_), reads, writes)

    def MSET(eng, ap, val, writes):
        return S.op(eng, lambda e: e.memset(ap, val), (), writes)

    def RSQ(out_ap, out_trk, in_ap, in_trks, scale=1.0, bias=0.0):
        ACT(out_ap, in_ap, AF.Ln, list(in_trks), [out_trk], bias=bias, scale=scale)
        ACT(out_ap, out_ap, AF.Exp, [out_trk], [out_trk], scale=-0.5)

    def TR(out_ap, in_ap, ident_ap, reads, writes):
        return S.op("pe", lambda e: e.transpose(out_ap, in_ap, ident_ap), reads, writes)

    k0 = lambda b: [b.k()]
    MSET("pool", ident_f.t[:], 1.0, k0(ident_f))
    S.op("pool", lambda e: e.affine_select(out=ident_f.t[:], in_=ident_f.t[:], pattern=[[-1, 128]],
                                           compare_op=ALU.is_equal, fill=0.0, base=0, channel_multiplier=1),
         k0(ident_f), k0(ident_f))
    CP("pool", ident_b.t[:], ident_f.t[:], k0(ident_f), k0(ident_b))
    MSET("pool", negmask.t[:], -30000.0, k0(negmask))
    S.op("pool", lambda e: e.affine_select(out=negmask.t[:], in_=negmask.t[:], pattern=[[1, 128]],
                                           compare_op=ALU.is_gt, fill=0.0, base=0, channel_multiplier=-1),
         k0(negmask), k0(negmask))
    smask_f = tmpf[2]
    MSET("pool", smask_f.t[:, 0:128], 1.0, k0(smask_f))
    S.op("pool", lambda e: e.affine_select(out=smask_f.t[:, 0:128], in_=smask_f.t[:, 0:128], pattern=[[-1, 128]],
                                           compare_op=ALU.is_gt, fill=0.0, base=0, channel_multiplier=1),
         k0(smask_f), k0(smask_f))
    CP("pool", smask.t[:], smask_f.t[:, 0:128], k0(smask_f), k0(smask))
    MSET("dve", onecol4.t[:], 0.0, k0(onecol4))
    for h in range(H):
        MSET("dve", onecol4.t[:, h, h:h + 1], 1.0, k0(onecol4))
    MSET("pool", onehot4.t[:], 1.0, k0(onehot4))
    for h in range(H):
        S.op("pool", lambda e, h=h: e.affine_select(out=onehot4.t[:, h, :], in_=onehot4.t[:, h, :], pattern=[[0, 128]],
                                                    compare_op=ALU.is_equal, fill=0.0, base=-h, channel_multiplier=1),
             k0(onehot4), k0(onehot4))
    MSET("dve", ones_row.t[:], 1.0, k0(ones_row))
    MSET("dve", mean1024.t[:], 1.0 / 1024.0, k0(mean1024))
    MSET("dve", mean256.t[:], 1.0 / 256.0, k0(mean256))
    MSET("dve", halfmc.t[:], 0.5, k0(halfmc))

    class _BV:
        def __init__(self, b):
            self.t = b.t[:, 0:128]
            self.k = b.k
    bdt = [_BV(tmpf[0]), _BV(tmpf[1])]
    prev = None
    for mi, bs_ in enumerate((16, 32, 64)):
        ng = 128 // bs_
        MSET("pool", esel.t[0:ng, :], 1.0, k0(esel))
        S.op("pool", lambda e, ng=ng, bs_=bs_: e.affine_select(out=esel.t[0:ng, :], in_=esel.t[0:ng, :], pattern=[[1, 128]],
                                                               compare_op=ALU.is_ge, fill=0.0, base=0, channel_multiplier=-bs_),
             k0(esel), k0(esel))
        S.op("pool", lambda e, ng=ng, bs_=bs_: e.affine_select(out=esel.t[0:ng, :], in_=esel.t[0:ng, :], pattern=[[-1, 128]],
                                                               compare_op=ALU.is_gt, fill=0.0, base=bs_, channel_multiplier=bs_),
             k0(esel), k0(esel))
        pbk = pm_ring.next()
        S.mm(pbk.t[:, 0:128], [(esel.t[0:ng, :], esel.t[0:ng, :])], k0(esel), [pbk.k()])
        cur_ = bdt[mi % 2]
        CP("dve", cur_.t[:], pbk.t[:, 0:128], [pbk.k()], k0(cur_))
        if prev is None:
            CP("dve", bmask[0].t[:], cur_.t[:], k0(cur_), k0(bmask[0]))
        else:
            TT("dve", bmask[mi].t[:], cur_.t[:], prev.t[:], ALU.subtract, k0(cur_) + k0(prev), k0(bmask[mi]))
        prev = cur_
    TSC("dve", bmask[3].t[:], prev.t[:], -1.0, 1.0, ALU.mult, ALU.add, k0(prev), k0(bmask[3]))

    wtrk = {}

    def conv_w(name, src, dst, rows):
        for l in range(DEPTH):
            for r0 in range(0, rows, 128):
                t = wtrk[(name, l, r0 // 128)] = Trk()
                S.dma(dst[l, r0:r0 + 128, :], src[l, r0:r0 + 128, :], (), [t], eng="pool")

    conv_w("ada", w_ada, wb_ada, D)
    conv_w("in", w_in, wb_in, D)
    conv_w("out", w_out, wb_out, D)
    conv_w("gu", w_gu, wb_gu, D)
    conv_w("dn", w_dn, wb_dn, D_FF)

    def load_rows_T(dst_ap, src_ap, R, dst_trk, evac="dve"):
        S.dma(stage.t[0:R, :], src_ap, (), k0(stage))
        pb, j = pg_ring.next()
        TR(pb.t[:, j, 0:R], stage.t[0:R, :], ident_f.t[0:R, 0:R], k0(stage) + k0(ident_f), [pb.k(j)])
        CP(evac, dst_ap, pb.t[:, j, 0:R], [pb.k(j)], [dst_trk])

    pk = pcols.k()
    load_rows_T(pcols.t[:, C_N1G:C_N1G + 16], norm1_g.rearrange("l (c p) -> (l c) p", p=128), 16, pk)
    load_rows_T(pcols.t[:, C_N2G:C_N2G + 16], norm2_g.rearrange("l (c p) -> (l c) p", p=128), 16, pk)
    load_rows_T(pcols.t[:, C_FNG:C_FNG + 8], final_g.rearrange("l (c p) -> (l c) p", p=128), 8, pk)
    load_rows_T(pcols.t[:, C_CBB:C_CBB + 4], conv_b_b.rearrange("l (c p) -> (l c) p", p=128), 4, pk)
    load_rows_T(pcols.t[:, C_LNG:C_LNG + 4], ln_b_g.rearrange("l (c p) -> (l c) p", p=128), 4, pk)
    load_rows_T(pcols.t[:, C_LNB:C_LNB + 4], ln_b_b.rearrange("l (c p) -> (l c) p", p=128), 4, pk)
    load_rows_T(pcols.t[:, C_GNG:C_GNG + 2], gdn_norm_g, 2, pk)
    load_rows_T(pcols.t[:, C_CAW:C_CAW + 12], conv_a_w.rearrange("l t (c p) -> (l t c) p", p=128), 12, pk)
    load_rows_T(pcols.t[:, C_CBW:C_CBW + 124], conv_b_w.rearrange("l t (c p) -> (l t c) p", p=128), 124, pk)
    load_rows_T(pcols.t[:, C_CQW:C_CQW + 96], conv_qkv_w.rearrange("l t (c p) -> (l t c) p", p=128), 96, pk)
    load_rows_T(bada.t[:, 0:96], b_ada.rearrange("l (c p) -> (l c) p", p=128), 96, bada.k())
    TSC("dve", pcols.t[:, C_LNGH:C_LNGH + 4], pcols.t[:, C_LNG:C_LNG + 4], 0.5, None, ALU.mult, None, [pk], [pk])
    TSC("dve", pcols.t[:, C_LNBH:C_LNBH + 4], pcols.t[:, C_LNB:C_LNB + 4], 0.5, None, ALU.mult, None, [pk], [pk])
    TSC("dve", pcols.t[:, C_GNG:C_GNG + 2], pcols.t[:, C_GNG:C_GNG + 2], 0.5, None, ALU.mult, None, [pk], [pk])

    with nc.allow_non_contiguous_dma(reason="tiny per-head params"):
        S.dma(hcols.t[:, 0:2], dt_bias.rearrange("l h -> h l"), (), k0(hcols))
        S.dma(hcols.t[:, 4:6], a_log.rearrange("l h -> h l"), (), k0(hcols))
    ACT(hcols.t[:, 2:4], hcols.t[:, 4:6], AF.Exp, k0(hcols), k0(hcols))
    TSC("dve", hcols.t[:, 2:4], hcols.t[:, 2:4], -1.0, None, ALU.mult, None, k0(hcols), k0(hcols))

    for c in range(KC):
        load_rows_T(cfm.t[:, c, 0:nstream], c_all[0:nstream, c * 128:(c + 1) * 128], nstream, cfm.k())
    tf = tmpf_ring.next()
    cfv = cfm.t[:, :, 0:nstream]
    tfv = tf.t[:, 0:KC * nstream].rearrange("p (c s) -> p c s", s=nstream)
    ACT(tfv, cfv, AF.Tanh, k0(cfm), k0(tf), scale=0.5)
    STT("dve", tfv, tfv, 1.0, cfv, ALU.add, ALU.mult, k0(tf) + k0(cfm), k0(tf))
    TSC("dve", cmod_b.t[:, :, 0:nstream], tfv, 0.5, None, ALU.mult, None, k0(tf), k0(cmod_b))

    pieces = []

    def add_piece(out_fn, in_ap, rtrk):
        pieces.append([(out_fn, in_ap, rtrk, (0, 1))])
        return len(pieces) - 1

    def add_piece2(subs):
        pieces.append(subs)
        return len(pieces) - 1

    def wk(slot):
        return [slot.k(0), slot.k(1)]

    class WS:
        issued = 0

    def w_issue_upto(idx):
        while WS.issued <= idx and WS.issued < len(pieces):
            slot = wslot[WS.issued % NW]
            for out_fn, in_ap, rtrk, keys in pieces[WS.issued]:
                S.dma(out_fn(slot.t), in_ap, rtrk, [slot.k(x) for x in keys])
            WS.issued += 1

    def w_get(idx, look=NW - 1):
        w_issue_upto(idx + look)
        return wslot[idx % NW]

    def kp(ap2d):
        return ap2d.rearrange("(k p) n -> p k n", p=128)

    ada_trk = lambda l: [wtrk[("ada", l, r)] for r in range(KC)]
    in_trk = lambda l: [wtrk[("in", l, r)] for r in range(KC)]
    out_trk = lambda l: [wtrk[("out", l, r)] for r in range(KC)]
    gu_trk = lambda l: [wtrk[("gu", l, r)] for r in range(KC)]
    cvtrk = [[Trk() for _ in range(4)] for _ in range(DEPTH)]

    ada_pieces = {}
    for l in range(DEPTH):
        for j in range(12):
            ada_pieces[(l, j)] = add_piece(lambda t: t[:, :, :], kp(wb_ada[l][:, j * 512:(j + 1) * 512]), ada_trk(l))
    for l in range(DEPTH):
        pmod = pm_ring.next()
        pmv = pmod.t[:, 0:48 * 8].rearrange("p (c s) -> p c s", s=8)
        for j in range(12):
            slot = w_get(ada_pieces[(l, j)])
            for m in range(4):
                oc = j * 4 + m
                S.mm(pmv[:, oc, 0:nstream],
                     [(slot.t[:, kc, m * 128:(m + 1) * 128], cmod_b.t[:, kc, 0:nstream]) for kc in range(KC)],
                     wk(slot) + k0(cmod_b), [pmod.k()])
        for s in range(nstream):
            TT("dve", modt.t[:, l, :, s], pmv[:, :, s], bada.t[:, l * 48:(l + 1) * 48], ALU.add,
               [pmod.k(), bada.k()], k0(modt))
        for s in range(nstream):
            for which, (sc0, sh0, g0, gcol) in enumerate(((8, 0, 16, C_N1G), (32, 24, 40, C_N2G))):
                base = which * 3
                STT("dve", mcol.t[:, l, base + 0, :, s], modt.t[:, l, sc0:sc0 + 8, s], 1.0,
                    pcols.t[:, gcol + l * 8:gcol + l * 8 + 8], ALU.add, ALU.mult, k0(modt) + [pk], k0(mcol))
                CP("dve", mcol.t[:, l, base + 1, :, s], modt.t[:, l, sh0:sh0 + 8, s], k0(modt), k0(mcol))
                CP("dve", mcol.t[:, l, base + 2, :, s], modt.t[:, l, g0:g0 + 8, s], k0(modt), k0(mcol))

    def cv_col(l, kind, c, t):
        if kind == "a":
            return C_CAW + (l * 3 + t) * 2 + c
        if kind == "b":
            return C_CBW + (l * 31 + t) * 2 + c
        return C_CQW + (l * 4 + t) * 12 + c

    for l in range(DEPTH):
        mats = [("a", c, t) for c in range(2) for t in range(3)] + \
               [("b", c, t) for c in range(2) for t in range(31)] + \
               [("q", c, t) for c in range(12) for t in range(4)]
        for g0 in range(0, len(mats), 32):
            grp = mats[g0:g0 + 32]
            slot = wslot[(g0 // 32) % NW]
            for i, (kind, c, t) in enumerate(grp):
                col = cv_col(l, kind, c, t)
                TSC("dve", slot.t[:, i // 4, (i % 4) * 128:(i % 4 + 1) * 128], ident_f.t[:], pcols.t[:, col:col + 1],
                    None, ALU.mult, None, k0(ident_f) + [pk], k0(slot))
            n = len(grp)
            slotv = slot.t[:].rearrange("p a (b c) -> p (a b) c", c=128)
            S.dma(wb_cv[l, g0:g0 + n].rearrange("m p c -> p m c"), slotv[:, 0:n, :], k0(slot), [cvtrk[l][g0 // 32]])

    if dbg == "stage0":
        d_mod = dout("d_mod", [128, DEPTH * 48 * 8])
        d_mcol = dout("d_mcol", [128, DEPTH * 6 * KC * 8])
        d_pcols = dout("d_pcols", [128, 304])
        for l in range(DEPTH):
            S.dma(d_mod[:, l * 384:(l + 1) * 384], zA.t[:, l, 0:384], k0(modt), ())
        S.dma(d_mcol, mcol.t[:].rearrange("p l a c s -> p (l a c s)"), k0(mcol), ())
        S.dma(d_pcols, pcols.t[:], [pk], ())
        S.finish()
        return

    ones4 = sb("ones4", [4, 128])
    MSET("dve", ones4.t[:], 1.0, k0(ones4))

    tiles = []
    if ns > 0:
        tiles.append(("S", 0, ns * TS, True, True))
    npt = tp // 512
    for i in range(npt):
        tiles.append((0, i * 512, 512, i == 0, i == npt - 1))

    def cv_out(n):
        return lambda t: t[:].rearrange("p a (b c) -> p (a b) c", c=128)[:, 0:n, :]

    def layer_pieces(l):
        P = {"cv": [], "in": [], "out": [[], []], "gu": [[], []], "dn": [[], []]}
        for j in range(7):
            ncol = min(512, D_IN - 512 * j)
            P["in"].append(add_piece(lambda t, ncol=ncol: t[:, :, 0:ncol], kp(wb_in[l][:, j * 512:j * 512 + ncol]), in_trk(l)))
        for j in range(4):
            n = min(32, NCV - 32 * j)
            P["cv"].append(add_piece(cv_out(n), wb_cv[l, 32 * j:32 * j + n].rearrange("m p c -> p m c"), cvtrk[l]))
        for hf in range(2):
            for j in range(2):
                P["out"][hf].append(add_piece(lambda t: t[:, :, :], kp(wb_out[l][:, j * 512:(j + 1) * 512]), out_trk(l)))
            for j in range(11):
                P["gu"][hf].append(add_piece2([
                    (lambda t: t[:, :, 0:256], kp(wb_gu[l][:, j * 256:(j + 1) * 256]), gu_trk(l), (0,)),
                    (lambda t: t[:, :, 256:512], kp(wb_gu[l][:, D_FF + j * 256:D_FF + (j + 1) * 256]), gu_trk(l), (1,))]))
            for j in range(8):
                P["dn"][hf].append(add_piece(
                    lambda t: t[:].rearrange("p a b -> p (a b)")[:, 0:FC * 128].rearrange("p (k c) -> p k c", c=128),
                    wb_dn[l][:, j * 128:(j + 1) * 128].rearrange("(k p) n -> p k n", p=128),
                    [wtrk[("dn", l, r)] for r in range(FC)]))
        return P

    tile_P = [[layer_pieces(l) for l in range(DEPTH)] for _ in tiles]

    alt = Ring(["act", "dve"])
    XK = lambda cs: xres.ks(cs)

    def rms_row(c0, c1, src_keys_fn, sq_src, nchunks, meanvec, eps, tag):
        for c in range(nchunks):
            ACT(ymix.t[:, c, c0:c1], sq_src(c), AF.Square, [src_keys_fn(c)], [ymix.k(c)])
        prow = pr_ring.next()
        S.mm(prow.t[0:1, c0:c1], [(meanvec.t[:, 0:1], ymix.t[:, c, c0:c1]) for c in range(nchunks)],
             ymix.ks(range(nchunks)) + k0(meanvec), [prow.k()])
        r0 = rows["t0"]
        RSQ(r0.t[0:1, c0:c1], r0.k(), prow.t[0:1, c0:c1], [prow.k()], bias=eps)
        pb = pm_ring.next()
        S.mm(pb.t[:, c0:c1], [(ones_row.t[0:1, :], r0.t[0:1, c0:c1])], [r0.k(), ones_row.k()], [pb.k()])
        return pb

    def norm_mod(c0, c1, l, which, segs):
        pb = rms_row(c0, c1, lambda c: xres.k(c), lambda c: xres.t[:, c, c0:c1], KC, mean1024, EPS, "n")
        for c in range(KC):
            tf = tmpf_ring.next()
            TT("dve", tf.t[:, c0:c1], xres.t[:, c, c0:c1], pb.t[:, c0:c1], ALU.mult, [xres.k(c), pb.k()], [tf.k()])
            for sidx, s0, n in segs:
                ACT(hbuf.t[:, c, s0:s0 + n], tf.t[:, s0:s0 + n], AF.Identity, [tf.k(), mcol.k()], [hbuf.k(c)],
                    bias=mcol.t[:, l, which * 3 + 1, c, sidx:sidx + 1], scale=mcol.t[:, l, which * 3 + 0, c, sidx:sidx + 1])

    def load_x(x_src, N):
        rb = min(128, N)
        for half in range(0, N, 256):
            nb = min(256, N - half)
            nblk = (nb + 127) // 128
            for b in range(nblk):
                S.dma(xtok.t[0:rb, b, :], x_src[half + b * 128:half + b * 128 + rb, :], (), [xtok.k(b)])
            for c in range(KC):
                p = pm_ring.next()
                for b in range(nblk):
                    TR(p.t[:, b * 128:b * 128 + rb], xtok.t[0:rb, b, c * 128:(c + 1) * 128], ident_f.t[0:rb, 0:rb],
                       [xtok.k(b), ident_f.k()], [p.k()])
                CP(alt.next(), xres.t[:, c, half:half + nb], p.t[:, 0:nb], [p.k()], [xres.k(c)])

    def store_y(y_dst, N):
        rb = min(128, N)
        for half in range(0, N, 256):
            nb = min(256, N - half)
            nblk = (nb + 127) // 128
            for b in range(nblk):
                t0_ = half + b * 128
                for cg in range(2):
                    p = pm_ring.next()
                    for c4 in range(4):
                        c = cg * 4 + c4
                        TR(p.t[0:rb, c4 * 128:(c4 + 1) * 128], xres.t[:, c, t0_:t0_ + rb], ident_f.t[:, :],
                           [xres.k(c), ident_f.k()], [p.k()])
                    CP(alt.next(), xtok.t[0:rb, b, cg * 512:(cg + 1) * 512], p.t[0:rb, 0:512], [p.k()], [xtok.k(b)])
                S.dma(y_dst[t0_:t0_ + rb, :], xtok.t[0:rb, b, :], [xtok.k(b)], ())

    def store_rows_T(dst_ap, src_ap, R, src_trk):
        pb, j = pg_ring.next()
        TR(pb.t[0:R, j, :], src_ap, ident_f.t[:, :], [src_trk, ident_f.k()], [pb.k(j)])
        CP("dve", stage.t[0:R, :], pb.t[0:R, j, :], [pb.k(j)], k0(stage))
        S.dma(dst_ap, stage.t[0:R, :], k0(stage), ())

    def run_gens(gens):
        gens = list(gens)
        while gens:
            for g in list(gens):
                try:
                    next(g)
                except StopIteration:
                    gens.remove(g)

    def layer(ti, l, stream, N, first, last):
        CKL[0] = l
        P = tile_P[ti][l]
        sample = (stream == "S")
        nseg = ns if sample else 1
        ns_ = N // nseg
        SEG = [(1 + q_, q_ * ns_, ns_) for q_ in range(nseg)] if sample else [(0, 0, N)]
        L = TS if sample else min(128, N)
        nblk = N // L

        def cin(buf, c, hw):
            return buf.t[:, c, 0:nseg * (hw + ns_)].rearrange("p (s w) -> p s w", w=hw + ns_)

        def cin_all(buf, hw):
            return buf.t[:, :, 0:nseg * (hw + ns_)].rearrange("p c (s w) -> p c s w", w=hw + ns_)

        def pv(ap):
            return ap.rearrange("p (s n) -> p s n", n=ns_)

        CP("dve", cin_all(ua, 2)[:, :, :, 0:2], hist_a.t[:, l, 0:nseg, :, :].rearrange("p s c t -> p c s t"), k0(hist_a), k0(ua))
        CP("dve", cin_all(gb, 30)[:, :, :, 0:30], hist_b.t[:, l, 0:nseg, :, :].rearrange("p s c t -> p c s t"), k0(hist_b), k0(gb))
        CP("dve", cin_all(qb, 3)[:, :, :, 0:3], hist_q.t[:, l, 0:nseg, :, :].rearrange("p s c t -> p c s t"), k0(hist_q), [qb.k("h")])

        ck(1)
        norm_mod(0, N, l, 0, SEG)
        ck(2)

        HK = hbuf.ks(range(KC))
        for j in range(7):
            slot = w_get(P["in"][j])
            nfull = min(4, 26 - 4 * j)
            for m in range(nfull):
                oc = j * 4 + m
                p = pm_ring.next()
                S.mm(p.t[:, 0:N], [(slot.t[:, kc, m * 128:(m + 1) * 128], hbuf.t[:, kc, 0:N]) for kc in range(KC)],
                     wk(slot) + HK, [p.k()])
                pN = p.t[:, 0:N]
                if oc < 4:
                    CP("act", zA.t[:, oc, 0:N], pN, [p.k()], [zA.k(oc)])
                elif oc < 6:
                    c = oc - 4
                    TT("dve", cin(ua, c, 2)[:, :, 2:2 + ns_], pv(p.t[:, 0:N]), pv(zA.t[:, c, 0:N]), ALU.mult, [p.k(), zA.k(c)], k0(ua))
                    if last:
                        for si, (_, c0, n) in enumerate(SEG):
                            TT("dve", tail_a.t[:, l, si, :, c], p.t[:, c0 + n - 2:c0 + n], zA.t[:, c, c0 + n - 2:c0 + n], ALU.mult,
                               [p.k(), zA.k(c)], k0(tail_a))
                elif oc < 8:
                    CP("act", zA.t[:, oc - 2, 0:N], pN, [p.k()], [zA.k(oc - 2)])
                elif oc < 10:
                    c = oc - 8
                    tf = tmpf_ring.next()
                    ACT(tf.t[:, 0:N], pN, AF.Tanh, [p.k()], [tf.k()], scale=0.5)
                    STT("dve", tf.t[:, 0:N], tf.t[:, 0:N], 1.0, zA.t[:, 4 + c, 0:N], ALU.add, ALU.mult,
                        [tf.k(), zA.k(4 + c)], [tf.k()])
                    ACT(cin(gb, c, 30)[:, :, 30:30 + ns_], pv(tf.t[:, 0:N]), AF.Copy, [tf.k()], k0(gb), scale=0.5)
                    if last:
                        for si, (_, c0, n) in enumerate(SEG):
                            TSC("dve", tail_b.t[:, l, si, :, c], tf.t[:, c0 + n - 30:c0 + n], 0.5, None, ALU.mult, None, [tf.k()], k0(tail_b))
                elif oc < 22:
                    c = oc - 10
                    CP(alt.next(), cin(qb, c, 3)[:, :, 3:3 + ns_], pv(pN), [p.k()], [qb.k()])
                    if last:
                        for si, (_, c0, n) in enumerate(SEG):
                            CP("dve", tail_q.t[:, l, si, :, c], p.t[:, c0 + n - 3:c0 + n], [p.k()], k0(tail_q))
                else:
                    hh = oc - 22
                    tf = tmpf_ring.next()
                    ACT(tf.t[:, 0:N], pN, AF.Tanh, [p.k()], [tf.k()], scale=0.5)
                    STT("dve", zg.t[:, hh, 0:N], tf.t[:, 0:N], 1.0, pN, ALU.add, ALU.mult, [tf.k(), p.k()], [zg.k(hh)])
            if j == 6:
                R = rows
                pa = pr_ring.next()
                S.mm(pa.t[0:4, 0:N], [(slot.t[:, kc, 256:260], hbuf.t[:, kc, 0:N]) for kc in range(KC)], wk(slot) + HK, [pa.k()])
                ACT(R["t0"].t[0:4, 0:N], pa.t[0:4, 0:N], AF.Exp, [pa.k(), hcols.k()], [R["t0"].k()], bias=hcols.t[:, l:l + 1])
                pbb = pr_ring.next()
                S.mm(pbb.t[0:4, 0:N], [(slot.t[:, kc, 260:264], hbuf.t[:, kc, 0:N]) for kc in range(KC)], wk(slot) + HK, [pbb.k()])
                ACT(R["c3"].t[0:4, 0:N], pbb.t[0:4, 0:N], AF.Tanh, [pbb.k()], [R["c3"].k()], scale=0.5)

        R = rows
        RK = lambda n: R[n].k()
        rv = lambda n: R[n].t[0:4, 0:N]
        ACT(rv("t0"), rv("t0"), AF.Ln, [RK("t0")], [RK("t0")], bias=1.0)
        TSC("dve", rv("g"), rv("t0"), hcols.t[:, 2 + l:3 + l], None, ALU.mult, None, [RK("t0"), hcols.k()], [RK("g")])
        TSC("dve", rv("c3"), rv("c3"), 0.5, 0.5, ALU.mult, ALU.add, [RK("c3")], [RK("c3")])
        ck(41)
        for b in range(nblk):
            bs = slice(b * L, (b + 1) * L)
            S.op("dve", lambda e, bs=bs: e.tensor_tensor_scan(out=R["gam"].t[0:4, bs], data0=ones4.t[0:4, 0:L],
                                                               data1=R["g"].t[0:4, bs], initial=0.0,
                                                               op0=ALU.mult, op1=ALU.add),
                 [RK("g"), ones4.k()], [RK("gam")])
        ck(42)
        TSC("dve", rv("ngam"), rv("gam"), -1.0, None, ALU.mult, None, [RK("gam")], [RK("ngam")])
        ACT(rv("egam"), rv("gam"), AF.Exp, [RK("gam")], [RK("egam")])
        TT("dve", rv("c1"), rv("c3"), rv("egam"), ALU.mult, [RK("c3"), RK("egam")], [RK("c1")])
        TSC("dve", rv("c4"), rv("c3"), -1.0, None, ALU.mult, None, [RK("c3")], [RK("c4")])
        TSC("dve", rv("c3"), rv("c3"), 0.5, None, ALU.mult, None, [RK("c3"), RK("c1"), RK("c4")], [RK("c3")])
        for b in range(nblk):
            bs = slice(b * L, (b + 1) * L)
            e_ = (b + 1) * L - 1
            ACT(R["g"].t[0:4, bs], R["gam"].t[0:4, bs], AF.Exp, [RK("gam")], [RK("g")],
                bias=R["gam"].t[0:4, e_:e_ + 1], scale=-1.0)
            CP("dve", egl_r.t[0:4, b:b + 1], R["egam"].t[0:4, e_:e_ + 1], [RK("egam")], k0(egl_r))
        ck(3)
        cvs = [w_get(P["cv"][g], look=3 - g) for g in range(4)]
        cvk = [t_ for x in cvs for t_ in wk(x)]

        def diag(mi):
            return cvs[mi // 32].t[:].rearrange("p a (b c) -> p (a b) c", c=128)[:, mi % 32, :]

        for c in range(2):
            p = pm_ring.next()
            S.mm(pv(p.t[:, 0:N]), [(diag(c * 3 + t), cin(ua, c, 2)[:, :, t:t + ns_]) for t in range(3)], cvk + k0(ua), [p.k()])
            TT("dve", ymix.t[:, c, 0:N], p.t[:, 0:N], zA.t[:, 2 + c, 0:N], ALU.mult, [p.k(), zA.k(2 + c)], [ymix.k(c)])
        if not sample:
            CP("dve", hist_a.t[:, l, 0, :, :], ua.t[:, :, N:N + 2], k0(ua), k0(hist_a))
        for c in range(12):
            p = pm_ring.next()
            S.mm(pv(p.t[:, 0:N]), [(diag(68 + c * 4 + t), cin(qb, c, 3)[:, :, t:t + ns_]) for t in range(4)],
                 cvk + [qb.k(), qb.k("h")], [p.k()])
            tf = tmpf_ring.next()
            ACT(tf.t[:, 0:N], p.t[:, 0:N], AF.Tanh, [p.k()], [tf.k()], scale=0.5)
            STT("dve", qkvs.t[:, c, 0:N], tf.t[:, 0:N], 1.0, p.t[:, 0:N], ALU.add, ALU.mult, [tf.k(), p.k()], [qkvs.k(c)])
        if not sample:
            CP("dve", hist_q.t[:, l, 0, :, :], qb.t[:, :, N:N + 3], [qb.k(), qb.k("h")], k0(hist_q))
        for c in range(2):
            p = pm_ring.next()
            S.mm(pv(p.t[:, 0:N]), [(diag(6 + c * 31 + t), cin(gb, c, 30)[:, :, t:t + ns_]) for t in range(31)], cvk + k0(gb), [p.k()])
            ACT(zA.t[:, 4 + c, 0:N], p.t[:, 0:N], AF.Identity, [p.k(), pk], [zA.k(4 + c)],
                bias=pcols.t[:, C_CBB + l * 2 + c:C_CBB + l * 2 + c + 1])
            CP("dve", ymix.t[:, 4 + c, 0:N], zA.t[:, 4 + c, 0:N], [zA.k(4 + c)], [ymix.k(4 + c)])
        if not sample:
            CP("dve", hist_b.t[:, l, 0, :, :], gb.t[:, :, N:N + 30], k0(gb), k0(hist_b))
        rm, ve = rows["t1"], rows["rk"]
        pr1 = pr_ring.next()
        S.mm(pr1.t[0:1, 0:N], [(mean256.t[:, 0:1], ymix.t[:, 4 + c, 0:N]) for c in range(2)],
             ymix.ks((4, 5)) + k0(mean256), [pr1.k()])
        CP("act", rm.t[0:1, 0:N], pr1.t[0:1, 0:N], [pr1.k()], [rm.k()])
        pbm = pm_ring.next()
        S.mm(pbm.t[:, 0:N], [(ones_row.t[0:1, :], rm.t[0:1, 0:N])], [rm.k(), ones_row.k()], [pbm.k()])
        for c in range(2):
            TT("dve", zA.t[:, 4 + c, 0:N], zA.t[:, 4 + c, 0:N], pbm.t[:, 0:N], ALU.subtract, [zA.k(4 + c), pbm.k()], [zA.k(4 + c)])
            ACT(ymix.t[:, 6 + c, 0:N], zA.t[:, 4 + c, 0:N], AF.Square, [zA.k(4 + c)], [ymix.k(6 + c)])
        pr2 = pr_ring.next()
        S.mm(pr2.t[0:1, 0:N], [(mean256.t[:, 0:1], ymix.t[:, 6 + c, 0:N]) for c in range(2)],
             ymix.ks((6, 7)) + k0(mean256), [pr2.k()])
        RSQ(ve.t[0:1, 0:N], ve.k(), pr2.t[0:1, 0:N], [pr2.k()], bias=EPS)
        pb1 = pm_ring.next()
        S.mm(pb1.t[:, 0:N], [(ones_row.t[0:1, :], ve.t[0:1, 0:N])], [ve.k(), ones_row.k()], [pb1.k()])
        for c in range(2):
            tf = tmpf_ring.next()
            TT("dve", tf.t[:, 0:N], zA.t[:, 4 + c, 0:N], pb1.t[:, 0:N], ALU.mult, [zA.k(4 + c), pb1.k()], [tf.k()])
            lc = l * 2 + c
            tf2 = tmpf_ring.next()
            ACT(tf2.t[:, 0:N], tf.t[:, 0:N], AF.Tanh, [tf.k(), pk], [tf2.k()],
                bias=pcols.t[:, C_LNBH + lc:C_LNBH + lc + 1], scale=pcols.t[:, C_LNGH + lc:C_LNGH + lc + 1])
            ACT(tf.t[:, 0:N], tf.t[:, 0:N], AF.Identity, [tf.k(), pk], [tf.k()],
                bias=pcols.t[:, C_LNBH + lc:C_LNBH + lc + 1], scale=pcols.t[:, C_LNGH + lc:C_LNGH + lc + 1])
            STT("dve", ymix.t[:, 2 + c, 0:N], tf2.t[:, 0:N], 1.0, tf.t[:, 0:N], ALU.add, ALU.mult,
                [tf.k(), tf2.k()], [ymix.k(2 + c)])

        ck(4)
        R = rows
        RK = lambda n: R[n].k()
        rv = lambda n: R[n].t[0:4, 0:N]
        ck(43)
        for base, name in ((0, "rq"), (4, "rk")):
            prn = pr_ring.next()
            for hh in range(H):
                ACT(ymix.t[:, 4 + hh, 0:N], qkvs.t[:, base + hh, 0:N], AF.Square, [qkvs.k(base + hh)], [ymix.k(4 + hh)])
            S.mm(prn.t[0:4, 0:N], [(onecol4.t[:, hh, :], ymix.t[:, 4 + hh, 0:N]) for hh in range(H)],
                 ymix.ks(range(4, 8)) + k0(onecol4), [prn.k()])
            RSQ(rv(name), RK(name), prn.t[0:4, 0:N], [prn.k()], bias=4 * EPS)
        ck(44)
        STT("dve", rv("rqe"), rv("rq"), DK ** -0.5, rv("egam"), ALU.mult, ALU.mult, [RK("rq"), RK("egam")], [RK("rqe")])
        TSC("dve", rv("rq"), rv("rq"), DK ** -0.5, None, ALU.mult, None, [RK("rq")], [RK("rq")])
        for hh in range(H):
            for rname, dst, src in (("rq", qn, hh), ("rqe", qe, hh), ("rk", kn, 4 + hh)):
                pbq = pm_ring.next()
                S.mm(pbq.t[:, 0:N], [(onehot4.t[:, hh, :], rv(rname))], [RK(rname), onehot4.k()], [pbq.k()])
                TT("dve", dst.t[:, hh, 0:N], qkvs.t[:, src, 0:N], pbq.t[:, 0:N], ALU.mult, [qkvs.k(src), pbq.k()], [dst.k(hh)])
            pgb, pj = pg_ring.next()
            S.mm(pgb.t[:, pj, 0:nblk], [(onehot4.t[:, hh, :], egl_r.t[0:4, 0:nblk])], k0(egl_r) + k0(onehot4), [pgb.k(pj)])
            CP("act", egl_c.t[:, hh, 0:nblk], pgb.t[:, pj, 0:nblk], [pgb.k(pj)], [egl_c.k(hh)])
        ck(45)
        for b in range(nblk):
            bs = slice(b * L, (b + 1) * L)
            pgb, pj = pg_ring.next()
            for q_, name in enumerate(("c1", "g", "c3", "c4")):
                TR(pgb.t[0:L, pj, 4 * q_:4 * q_ + 4], R[name].t[0:4, bs], ident_f.t[0:4, 0:4],
                   [RK(name), ident_f.k()], [pgb.k(pj)])
            CP("act", cols[b].t[0:L, 0:16], pgb.t[0:L, pj, 0:16], [pgb.k(pj)], k0(cols[b]))

        ck(5)
        bank_ring = Ring(ph)
        phbv = {id(ph[i]): phb[i] for i in range(4)}
        SKs = [S_b.k((l, hh)) for hh in range(H)]
        SFKs = [S_f.k((l, hh)) for hh in range(H)]
        Sf4 = S_f.t[:, l, :, :]
        Sb4 = S_b.t[:, l, :, :]

        def gdn_chain(b, ci, h0, h1, banks):
            nh = h1 - h0
            c0_ = b * L
            cs = slice(c0_, c0_ + L)
            col = cols[b]
            ck_ = col.k()
            HS = range(h0, h1)
            KK = lambda t4: [t4.k(ci)]
            v3 = lambda t4: t4.t[0:L, h0:h1, 0:L]
            bc_col = lambda q_: col.t[0:L, q_ * 4 + h0:q_ * 4 + h1].unsqueeze(2).to_broadcast([L, nh, 128])
            bc_colL = lambda q_: col.t[0:L, q_ * 4 + h0:q_ * 4 + h1].unsqueeze(2).to_broadcast([L, nh, L])
            bc_m = lambda mk: mk.t[0:L, 0:L].unsqueeze(1).to_broadcast([L, nh, L])
            Ib = ident_b.t[0:L, 0:L]
            If = ident_f.t[0:L, 0:L]
            bring = Ring(banks)

            def pbank():
                bk = bring.next()
                return bk, phbv[id(bk)]

            sks = [SKs[hh] for hh in HS]
            sfks = [SFKs[hh] for hh in HS]
            sf = S_f.t[:, l, h0:h1, :]
            sbv = S_b.t[:, l, h0:h1, :]
            bA, _ = pbank()
            for hh in HS:
                S.mm(bA.t[0:L, hh, 0:L], [(kn.t[:, hh, cs], kn.t[:, hh, cs])], [kn.k(hh)], [bA.k()])
            bB, _ = pbank()
            for hh in HS:
                S.mm(bB.t[0:L, hh, 0:L], [(qn.t[:, hh, cs], kn.t[:, hh, cs])], [kn.k(hh), qn.k(hh)], [bB.k()])
            yield
            CP("act", v3(G_X0), bA.t[0:L, h0:h1, 0:L], [bA.k()], KK(G_X0))
            CP("dve", v3(G_P), bB.t[0:L, h0:h1, 0:L], [bB.k()], KK(G_P))
            bC, _ = pbank()
            for hh in HS:
                S.mm(bC.t[0:L, hh, 0:L], [(R["gam"].t[0:4, cs], onehot4.t[:, hh, 0:L]),
                                          (onehot4.t[:, hh, 0:L], R["ngam"].t[0:4, cs]),
                                          (If, negmask.t[0:L, 0:L])],
                     [RK("gam"), RK("ngam"), onehot4.k(), ident_f.k(), negmask.k()], [bC.k()])
            yield
            ACT(v3(G_D), bC.t[0:L, h0:h1, 0:L], AF.Exp, [bC.k()], KK(G_D))
            yield
            TT("dve", v3(G_P), v3(G_P), v3(G_D), ALU.mult, KK(G_P) + KK(G_D), KK(G_P))
            TT("dve", v3(G_Dn), v3(G_D), bc_colL(3), ALU.mult, KK(G_D) + [ck_], KK(G_Dn))
            TT("dve", v3(G_Dn), v3(G_Dn), bc_m(smask), ALU.mult, KK(G_Dn) + k0(smask), KK(G_Dn))
            TT("dve", v3(G_X0), v3(G_X0), v3(G_Dn), ALU.mult, KK(G_X0) + KK(G_Dn), KK(G_X0))
            yield
            bT1, vT1 = pbank()
            for hh in HS:
                TR(vT1[0:L, hh, 0:L], G_X0.t[0:L, hh, 0:L], Ib, KK(G_X0) + [ident_b.k()], [bT1.k()])
            bT2, vT2 = pbank()
            for hh in HS:
                TR(vT2[0:L, hh, 0:L], G_P.t[0:L, hh, 0:L], Ib, KK(G_P) + [ident_b.k()], [bT2.k()])
            nmerge = 3 if L == 128 else 2
            TT("dve", v3(G_X[0]), v3(G_X0), bc_m(bmask[0]), ALU.mult, KK(G_X0) + k0(bmask[0]), KK(G_X[0]))
            for m in range(nmerge):
                TT("pool", v3(G_Xo[m]), v3(G_X0), bc_m(bmask[m + 1]), ALU.mult, KK(G_X0) + k0(bmask[m + 1]), KK(G_Xo[m]))
            yield
            TT("dve", v3(G_Y[0]), vT1[0:L, h0:h1, 0:L], bc_m(bmask[0]), ALU.mult, [bT1.k()] + k0(bmask[0]), KK(G_Y[0]))
            CP("act", v3(G_PT), vT2[0:L, h0:h1, 0:L], [bT2.k()], KK(G_PT))
            yield
            TT("dve", v3(G_Q[0]), v3(G_Y[0]), Ib.unsqueeze(1).to_broadcast([L, nh, L]), ALU.add,
               KK(G_Y[0]) + [ident_b.k()], KK(G_Q[0]))
            yield
            cur, qc = 0, 0
            X, Y = G_X, G_Y
            for k in range(1, 4):
                nx = 1 - cur
                bX, _ = pbank()
                for hh in HS:
                    S.mm(bX.t[0:L, hh, 0:L], [(Y[cur].t[0:L, hh, 0:L], X[cur].t[0:L, hh, 0:L])],
                         KK(X[cur]) + KK(Y[cur]), [bX.k()])
                if k < 3:
                    bY, _ = pbank()
                    for hh in HS:
                        S.mm(bY.t[0:L, hh, 0:L], [(X[cur].t[0:L, hh, 0:L], Y[cur].t[0:L, hh, 0:L])],
                             KK(X[cur]) + KK(Y[cur]), [bY.k()])
                yield
                CP("act", v3(X[nx]), bX.t[0:L, h0:h1, 0:L], [bX.k()], KK(X[nx]))
                if k < 3:
                    CP("dve", v3(Y[nx]), bY.t[0:L, h0:h1, 0:L], [bY.k()], KK(Y[nx]))
                yield
                bQ, _ = pbank()
                for hh in HS:
                    S.mm(bQ.t[0:L, hh, 0:L], [(Ib, G_Q[qc].t[0:L, hh, 0:L]), (X[nx].t[0:L, hh, 0:L], G_Q[qc].t[0:L, hh, 0:L])],
                         [ident_b.k()] + KK(G_Q[qc]) + KK(X[nx]), [bQ.k()])
                yield
                CP(alt.next(), v3(G_Q[1 - qc]), bQ.t[0:L, h0:h1, 0:L], [bQ.k()], KK(G_Q[1 - qc]))
                qc = 1 - qc
                cur = nx
                yield
            for m in range(nmerge):
                bT, vT = pbank()
                for hh in HS:
                    TR(vT[0:L, hh, 0:L], G_Q[qc].t[0:L, hh, 0:L], Ib, KK(G_Q[qc]) + [ident_b.k()], [bT.k()])
                bV, _ = pbank()
                for hh in HS:
                    S.mm(bV.t[0:L, hh, 0:L], [(G_Xo[m].t[0:L, hh, 0:L], G_Q[qc].t[0:L, hh, 0:L])],
                         KK(G_Xo[m]) + KK(G_Q[qc]), [bV.k()])
                yield
                CP("act", v3(G_T), vT[0:L, h0:h1, 0:L], [bT.k()], KK(G_T))
                CP("dve", v3(G_V), bV.t[0:L, h0:h1, 0:L], [bV.k()], KK(G_V))
                yield
                bQ, _ = pbank()
                for hh in HS:
                    S.mm(bQ.t[0:L, hh, 0:L], [(Ib, G_Q[qc].t[0:L, hh, 0:L]), (G_T.t[0:L, hh, 0:L], G_V.t[0:L, hh, 0:L])],
                         [ident_b.k()] + KK(G_Q[qc]) + KK(G_T) + KK(G_V), [bQ.k()])
                yield
                CP(alt.next(), v3(G_Q[1 - qc]), bQ.t[0:L, h0:h1, 0:L], [bQ.k()], KK(G_Q[1 - qc]))
                qc = 1 - qc
                yield
            Qf = G_Q[qc]
            bK, vK = pbank()
            for hh in HS:
                TR(vK[0:L, hh, 0:128], kn.t[:, hh, cs], ident_b.t[:, :], [kn.k(hh), ident_b.k()], [bK.k()])
            bVv, vVv = pbank()
            for hh in HS:
                TR(vVv[0:L, hh, 0:128], qkvs.t[:, 8 + hh, cs], ident_b.t[:, :], [qkvs.k(8 + hh), ident_b.k()], [bVv.k()])
            yield
            TT("dve", G_kw.t[0:L, h0:h1, :], vK[0:L, h0:h1, 0:128], bc_col(0), ALU.mult, [bK.k(), ck_], KK(G_kw))
            TT("dve", G_kh.t[0:L, h0:h1, :], vK[0:L, h0:h1, 0:128], bc_col(1), ALU.mult, [bK.k(), ck_], KK(G_kh))
            TT("dve", G_vb.t[0:L, h0:h1, :], vVv[0:L, h0:h1, 0:128], bc_col(2), ALU.mult, [bVv.k(), ck_], KK(G_vb))
            yield
            bW, _ = pbank()
            for hh in HS:
                S.mm(bW.t[:, hh, 0:L], [(G_kw.t[0:L, hh, :], Qf.t[0:L, hh, 0:L])], KK(G_kw) + KK(Qf), [bW.k()])
            yield
            ACT(G_nw.t[:, h0:h1, 0:L], bW.t[:, h0:h1, 0:L], AF.Copy, [bW.k()], KK(G_nw), scale=-1.0)
            yield
            bVn, _ = pbank()
            for hh in HS:
                S.mm(bVn.t[0:L, hh, :], [(Qf.t[0:L, hh, 0:L], G_vb.t[0:L, hh, :]), (G_nw.t[:, hh, 0:L], S_b.t[:, l, hh, :])],
                     KK(Qf) + KK(G_vb) + KK(G_nw) + [SKs[hh]], [bVn.k()])
            yield
            CP("dve", G_vn.t[0:L, h0:h1, :], bVn.t[0:L, h0:h1, :], [bVn.k()], KK(G_vn))
            yield
            bO, _ = pbank()
            for hh in HS:
                S.mm(bO.t[0:L, hh, :], [(qe.t[:, hh, cs], S_b.t[:, l, hh, :]), (G_PT.t[0:L, hh, 0:L], G_vn.t[0:L, hh, :])],
                     [qe.k(hh), SKs[hh]] + KK(G_PT) + KK(G_vn), [bO.k()])
            bS, _ = pbank()
            for hh in HS:
                S.mm(bS.t[:, hh, :], [(G_kh.t[0:L, hh, :], G_vn.t[0:L, hh, :])], KK(G_kh) + KK(G_vn), [bS.k()])
            TT("dve", sf, sf, egl_c.t[:, h0:h1, b:b + 1].to_broadcast([128, nh, 128]), ALU.mult,
               sfks + egl_c.ks(HS), sfks)
            yield
            ACT(G_T.t[0:L, h0:h1, :], bO.t[0:L, h0:h1, :], AF.Square, [bO.k()], KK(G_T))
            TT("dve", sf, sf, bS.t[:, h0:h1, :], ALU.add, sfks + [bS.k()], sfks)
            yield
            CP("act", sbv, sf, sfks, sks)
            sc0 = ci * 8
            S.op("dve", lambda e: e.reduce_sum(out=G_sc.t[0:L, sc0:sc0 + nh], in_=G_T.t[0:L, h0:h1, :], axis=mybir.AxisListType.X),
                 KK(G_T), KK(G_sc))
            yield
            RSQ(G_sc.t[0:L, sc0 + 4:sc0 + 4 + nh], G_sc.k(ci), G_sc.t[0:L, sc0:sc0 + nh], KK(G_sc), scale=1.0 / DK, bias=EPS)
            yield
            TT("dve", G_on.t[0:L, h0:h1, :], bO.t[0:L, h0:h1, :],
               G_sc.t[0:L, sc0 + 4:sc0 + 4 + nh].unsqueeze(2).to_broadcast([L, nh, 128]), ALU.mult,
               [bO.k()] + KK(G_sc), KK(G_on))
            yield
            bF, vF = pbank()
            for hh in HS:
                TR(vF[:, hh, 0:L], G_on.t[0:L, hh, :], Ib, KK(G_on) + [ident_b.k()], [bF.k()])
            yield
            STT("dve", ymix.t[:, 4 + h0:4 + h1, cs], vF[:, h0:h1, 0:L], pcols.t[:, C_GNG + l:C_GNG + l + 1], zg.t[:, h0:h1, cs],
                ALU.mult, ALU.mult, [bF.k(), pk] + zg.ks(HS), ymix.ks(range(4 + h0, 4 + h1)))

        def gdn_blocks(blocks):
            for b in blocks:
                if sample:
                    for hh in range(H):
                        S.dma(S_f.t[:, l, hh, :], st_s[l, b, hh], (), [S_f.k((l, hh))])
                        CP("act", S_b.t[:, l, hh, :], S_f.t[:, l, hh, :], [S_f.k((l, hh))], [S_b.k((l, hh))])
                gens = [gdn_chain(b, 0, 0, 2, [ph[0], ph[1]]), gdn_chain(b, 1, 2, 4, [ph[2], ph[3]])]
                while gens:
                    for g in list(gens):
                        try:
                            next(g)
                        except StopIteration:
                            gens.remove(g)
                    yield
                if sample:
                    for hh in range(H):
                        S.dma(ns_s[l, b, hh], S_f.t[:, l, hh, :], [S_f.k((l, hh))], ())

        def post(c0, c1, hf):
            segs = [(sidx, max(s0, c0), min(s0 + n, c1) - max(s0, c0)) for (sidx, s0, n) in SEG if s0 < c1 and s0 + n > c0]
            YK = ymix.ks(range(KC))
            for j in range(2):
                slot = w_get(P["out"][hf][j])
                for m in range(4):
                    oc = j * 4 + m
                    p = pm_ring.next()
                    yield from S.mm_gen(p.t[:, c0:c1], [(slot.t[:, kc, m * 128:(m + 1) * 128], ymix.t[:, kc, c0:c1]) for kc in range(KC)],
                                        wk(slot) + YK, [p.k()])
                    for sidx, s0, n in segs:
                        STT("dve", xres.t[:, oc, s0:s0 + n], p.t[:, s0:s0 + n], mcol.t[:, l, 2, oc, sidx:sidx + 1], xres.t[:, oc, s0:s0 + n],
                            ALU.mult, ALU.add, [p.k(), mcol.k(), xres.k(oc)], [xres.k(oc)])
                    yield
            norm_mod(c0, c1, l, 1, segs)
            yield
            for j in range(11):
                slot = w_get(P["gu"][hf][j])
                for m2 in range(2):
                    fc = 2 * j + m2
                    pgt = pm_ring.next()
                    yield from S.mm_gen(pgt.t[:, c0:c1], [(slot.t[:, kc, m2 * 128:(m2 + 1) * 128], hbuf.t[:, kc, c0:c1]) for kc in range(KC)],
                                        wk(slot) + HK, [pgt.k()])
                    yield
                    put = pm_ring.next()
                    yield from S.mm_gen(put.t[:, c0:c1], [(slot.t[:, kc, 256 + m2 * 128:256 + (m2 + 1) * 128], hbuf.t[:, kc, c0:c1]) for kc in range(KC)],
                                        wk(slot) + HK, [put.k()])
                    tf = tmpf_ring.next()
                    ACT(tf.t[:, c0:c1], pgt.t[:, c0:c1], AF.Tanh, [pgt.k()], [tf.k()], scale=0.5)
                    STT("dve", tf.t[:, c0:c1], tf.t[:, c0:c1], 1.0, pgt.t[:, c0:c1], ALU.add, ALU.mult, [tf.k(), pgt.k()], [tf.k()])
                    STT("dve", ffa.t[:, fc, c0:c1], tf.t[:, c0:c1], 0.5, put.t[:, c0:c1], ALU.mult, ALU.mult, [tf.k(), put.k()], [ffa.k()])
                    yield
            for oc in range(KC):
                slot = w_get(P["dn"][hf][oc])
                sv = slot.t[:].rearrange("p a b -> p (a b)")[:, 0:FC * 128].rearrange("p (k c) -> p k c", c=128)
                p = pm_ring.next()
                yield from S.mm_gen(p.t[:, c0:c1], [(sv[:, k, :], ffa.t[:, k, c0:c1]) for k in range(FC)], wk(slot) + [ffa.k()], [p.k()])
                for sidx, s0, n in segs:
                    STT("dve", xres.t[:, oc, s0:s0 + n], p.t[:, s0:s0 + n], mcol.t[:, l, 5, oc, sidx:sidx + 1], xres.t[:, oc, s0:s0 + n],
                        ALU.mult, ALU.add, [p.k(), mcol.k(), xres.k(oc)], [xres.k(oc)])
                yield

        hN, hB = N // 2, nblk // 2
        for _ in gdn_blocks(range(0, hB)):
            pass
        ck(6)
        run_gens([gdn_blocks(range(hB, nblk)), post(0, hN, 0)])
        ck(7)
        for _ in post(hN, N, 1):
            pass
        ck(8)

    def init_stream(stream):
        if stream == 0:
            for hb in (hist_a, hist_b, hist_q):
                MSET("pool", hb.t[:], 0.0, k0(hb))
            for l in range(DEPTH):
                for hh in range(H):
                    MSET("pool", S_f.t[:, l, hh, :], 0.0, [S_f.k((l, hh))])
                    MSET("pool", S_b.t[:, l, hh, :], 0.0, [S_b.k((l, hh))])
            return
        for l in range(DEPTH):
            for s_ in range(ns):
                load_rows_T(hist_a.t[:, l, s_, :, :].rearrange("p c t -> p t c"),
                            st_a[l, s_].rearrange("t (c p) -> (t c) p", p=128), 4, hist_a.k())
                load_rows_T(hist_b.t[:, l, s_, :, :].rearrange("p c t -> p t c"),
                            st_b[l, s_].rearrange("t (c p) -> (t c) p", p=128), 60, hist_b.k())
                load_rows_T(hist_q.t[:, l, s_, :, :].rearrange("p c t -> p t c"),
                            st_q[l, s_].rearrange("t (c p) -> (t c) p", p=128), 36, hist_q.k())

    def finish_stream(stream):
        for l in range(DEPTH):
            slots = [(0, na_p[l], nb_p[l], nq_p[l])] if stream == 0 else \
                [(s_, na_s[l, s_], nb_s[l, s_], nq_s[l, s_]) for s_ in range(ns)]
            for si, da, db, dq in slots:
                store_rows_T(da.rearrange("t (c p) -> (t c) p", p=128), tail_a.t[:, l, si, :, :].rearrange("p t c -> p (t c)"), 4, tail_a.k())
                store_rows_T(db.rearrange("t (c p) -> (t c) p", p=128), tail_b.t[:, l, si, :, :].rearrange("p t c -> p (t c)"), 60, tail_b.k())
                store_rows_T(dq.rearrange("t (c p) -> (t c) p", p=128), tail_q.t[:, l, si, :, :].rearrange("p t c -> p (t c)"), 36, tail_q.k())
            if stream == 0:
                for hh in range(H):
                    S.dma(ns_p[l][hh], S_f.t[:, l, hh, :], [S_f.k((l, hh))], ())

    def _main_body():
        for ti, (stream, tok0, N, first, last) in enumerate(tiles):
            if first:
                init_stream(stream)
            x_src = xp[tok0:tok0 + N, :] if stream == 0 else xs.rearrange("s t d -> (s t) d")[0:N, :]
            load_x(x_src, N)
            for l in range(DEPTH):
                layer(ti, l, stream, N, first, last)
            pb = rms_row(0, N, lambda c: xres.k(c), lambda c: xres.t[:, c, 0:N], KC, mean1024, EPS, "f")
            for c in range(KC):
                tf = tmpf_ring.next()
                TT("dve", tf.t[:, 0:N], xres.t[:, c, 0:N], pb.t[:, 0:N], ALU.mult, [xres.k(c), pb.k()], [tf.k()])
                ACT(xres.t[:, c, 0:N], tf.t[:, 0:N], AF.Copy, [tf.k(), pk], [xres.k(c)], scale=pcols.t[:, C_FNG + c:C_FNG + c + 1])
            y_dst = y_p[tok0:tok0 + N, :] if stream == 0 else y_s.rearrange("s t d -> (s t) d")[0:N, :]
            store_y(y_dst, N)
            if last:
                finish_stream(stream)

    try:
        _main_body()
    except _Stop:
        pass
    S.finish()
    nc._n_ins = S.n_ins
    nc._cnt = dict(S.cnt)
    nc._dcnt = {q: list(v) for q, v in S.dcnt.items()}


W_NAMES = ["norm1_g", "w_ada", "b_ada", "w_in", "conv_a_w", "conv_b_w", "conv_b_b", "ln_b_g", "ln_b_b",
           "conv_qkv_w", "a_log", "dt_bias", "gdn_norm_g", "w_out", "norm2_g", "w_gate_up", "w_down"]


def make_in_maps(inp, tp=TP, n_cores=8):
    f = lambda a: np.ascontiguousarray(np.asarray(a, dtype=np.float32))
    shared = {n: f(inp[n]) for n in W_NAMES}
    shared["final_norm_g"] = f(inp["final_norm_g"]).reshape(1, D)
    maps = []
    for i in range(n_cores):
        m = dict(shared)
        m["xp"] = f(inp["x_prompt"][i, :tp])
        sl = slice(NS * i, NS * (i + 1))
        m["xs"] = f(inp["x_sample"][sl])
        m["st_a"] = f(inp["state_conv_a"][:, sl])
        m["st_b"] = f(inp["state_conv_b"][:, sl])
        m["st_q"] = f(inp["state_conv_qkv"][:, sl])
        m["st_s"] = f(inp["state_gdn"][:, sl])
        m["c_all"] = f(np.concatenate([inp["c_prompt"][i:i + 1], inp["c_sample"][sl]], axis=0))
        maps.append(m)
    return maps


_NC_CACHE = {}


def _get_nc(tp=TP):
    if tp not in _NC_CACHE:
        _NC_CACHE[tp] = build(tp=tp)
    return _NC_CACHE[tp]


def assemble(results, tp=TP):
    n = len(results)
    g = lambda k: [np.asarray(r[k], dtype=np.float32) for r in results]
    y_p = np.stack(g("y_p"), 0)
    y_s = np.concatenate(g("y_s"), 0)
    pa = np.stack(g("na_p"), 1)
    pb = np.stack(g("nb_p"), 1)
    pq = np.stack(g("nq_p"), 1)
    ps_ = np.stack(g("ns_p"), 1)
    sa = np.concatenate(g("na_s"), 1)
    sb_ = np.concatenate(g("nb_s"), 1)
    sq = np.concatenate(g("nq_s"), 1)
    ss = np.concatenate(g("ns_s"), 1)
    return (y_p, y_s, pa, pb, pq, ps_, sa, sb_, sq, ss)


def kernel(**inputs):
    nc = _get_nc(TP)
    in_maps = make_in_maps(inputs, TP, 8)
    res = run_bass_kernel_spmd(nc, in_maps, core_ids=list(range(8)))
    return assemble(res.results, TP)
```

```python
import numpy as np
from contextlib import ExitStack
import concourse.bass as bass
import concourse.mybir as mybir
from concourse.bass_utils import run_bass_kernel_spmd

F32 = mybir.dt.float32
BF16 = mybir.dt.bfloat16
ALU = mybir.AluOpType
AF = mybir.ActivationFunctionType

D = 1024
DEPTH = 2
TP = 8192
NS = 4
TS = 64
D_A = 256
D_B = 256
D_C = 512
H = 4
DK = 128
D_FF = 2816
D_IN = 3336
EPS = 1e-6
NSTREAM = 1 + NS
KC = D // 128
FC = D_FF // 128
NCV = 116
CV_A, CV_B, CV_Q = 0, 6, 68


class Trk:
    __slots__ = ("w", "r", "excl")

    def __init__(self, excl=False):
        self.w = None
        self.r = {}
        self.excl = excl


def _split(reads, writes):
    xr = [r for r in reads if r.excl]
    if not xr:
        return reads, writes
    return [r for r in reads if not r.excl], list(writes) + xr


class Buf:
    def __init__(self, t):
        self.t = t
        self._k = {}

    def k(self, key=0):
        tr = self._k.get(key)
        if tr is None:
            tr = self._k[key] = Trk()
        return tr

    def ks(self, keys):
        return [self.k(x) for x in keys]


class BankBuf(Buf):
    def k(self, key=0):
        tr = self._k.get(0)
        if tr is None:
            tr = self._k[0] = Trk(excl=True)
        return tr


class Sched:
    ENG = ("pe", "dve", "act", "pool", "sp")

    def __init__(self, nc, es):
        self.nc = nc
        self.e = {"pe": nc.tensor, "dve": nc.vector, "act": nc.scalar, "pool": nc.gpsimd, "sp": nc.sync}
        self.sem = {k: es.enter_context(nc.semaphore("sem_" + k)) for k in ("pe", "dve", "act", "pool")}
        self.cnt = {k: 0 for k in self.sem}
        self.waited = {k: {} for k in self.ENG}
        self.ndma = 24
        self.dsem = {q: [es.enter_context(nc.semaphore("dsem_%s%d" % (q, i))) for i in range(self.ndma)]
                     for q in ("sp", "pool")}
        self.dcnt = {q: [0] * self.ndma for q in ("sp", "pool")}
        self.dnext = {q: 0 for q in ("sp", "pool")}
        self.n_ins = 0

    def _wait(self, eng, deps):
        w = self.waited[eng]
        best = {}
        for d in deps:
            if d is None:
                continue
            sem, val = d
            if val > best.get(id(sem), (None, 0))[1]:
                best[id(sem)] = (sem, val)
        for sem, val in best.values():
            if w.get(id(sem), 0) < val:
                self.e[eng].wait_ge(sem, val)
                w[id(sem)] = val

    def _deps(self, eng, reads, writes):
        deps = []
        mysem = self.sem.get(eng)
        for r in reads:
            deps.append(r.w)
        for t in writes:
            skip_own = (eng == "pe" and t.excl)
            if t.w is not None and not (skip_own and t.w[0] is mysem):
                deps.append(t.w)
            for d in t.r.values():
                if not (skip_own and d[0] is mysem):
                    deps.append(d)
        return deps

    def op(self, eng, fn, reads=(), writes=()):
        reads, writes = _split(reads, writes)
        self._wait(eng, self._deps(eng, reads, writes))
        ins = fn(self.e[eng])
        self.cnt[eng] += 1
        ins.then_inc(self.sem[eng], 1)
        me = (self.sem[eng], self.cnt[eng])
        for r in reads:
            r.r[eng] = me
        for t in writes:
            t.w = me
            t.r = {}
        self.n_ins += 1
        return me

    def mm(self, out_ap, pairs, reads, writes, transpose=False):
        reads, writes = _split(reads, writes)
        self._wait("pe", self._deps("pe", reads, writes))
        pe = self.e["pe"]
        n = len(pairs)
        ins = None
        for i, (l, r) in enumerate(pairs):
            ins = pe.matmul(out_ap, lhsT=l, rhs=r, start=(i == 0), stop=(i == n - 1))
        self.cnt["pe"] += 1
        ins.then_inc(self.sem["pe"], 1)
        me = (self.sem["pe"], self.cnt["pe"])
        for r in reads:
            r.r["pe"] = me
        for t in writes:
            t.w = me
            t.r = {}
        self.n_ins += n
        return me

    def mm_gen(self, out_ap, pairs, reads, writes, chunk=4):
        reads, writes = _split(reads, writes)
        self._wait("pe", self._deps("pe", reads, writes))
        pe = self.e["pe"]
        n = len(pairs)
        ins = None
        for i, (l, r) in enumerate(pairs):
            ins = pe.matmul(out_ap, lhsT=l, rhs=r, start=(i == 0), stop=(i == n - 1))
            if (i + 1) % chunk == 0 and i != n - 1:
                yield
        self.cnt["pe"] += 1
        ins.then_inc(self.sem["pe"], 1)
        me = (self.sem["pe"], self.cnt["pe"])
        for r in reads:
            r.r["pe"] = me
        for t in writes:
            t.w = me
            t.r = {}
        self.n_ins += n

    def dma(self, out_ap, in_ap, reads=(), writes=(), eng="sp"):
        i = self.dnext[eng]
        self.dnext[eng] = (i + 1) % self.ndma
        sem = self.dsem[eng][i]
        cnt = self.dcnt[eng]
        deps = self._deps(eng, reads, writes)
        if cnt[i] > 0:
            deps.append((sem, cnt[i]))
        self._wait(eng, deps)
        cnt[i] += 16
        self.e[eng].dma_start(out=out_ap, in_=in_ap).then_inc(sem, 16)
        me = (sem, cnt[i])
        for r in reads:
            r.r["dma_%s%d" % (eng, i)] = me
        for t in writes:
            t.w = me
            t.r = {}
        self.n_ins += 1
        return me

    def finish(self):
        deps = [(self.dsem[q][i], self.dcnt[q][i]) for q in self.dsem for i in range(self.ndma) if self.dcnt[q][i] > 0]
        deps += [(self.sem[k], self.cnt[k]) for k in self.sem if self.cnt[k] > 0]
        self._wait("sp", deps)


class Ring:
    def __init__(self, items):
        self.items = items
        self.i = 0

    def next(self):
        it = self.items[self.i]
        self.i = (self.i + 1) % len(self.items)
        return it


class _Stop(Exception):
    pass


def build(tp=TP, ns=NS, dbg=False):
    nc = bass.Bass("TRN2", target_bir_lowering=False)
    es = ExitStack()
    with es:
        _build(nc, es, tp, ns, dbg)
    return nc


def _build(nc, es, tp, ns, dbg):
    S = Sched(nc, es)
    nstream = 1 + ns
    stop_at = dbg if isinstance(dbg, int) and not isinstance(dbg, bool) else None

    CKL = [0]

    def ck(n):
        if stop_at is not None and n + 100 * CKL[0] == stop_at:
            raise _Stop()

    def din(name, shape):
        return nc.dram_tensor(name, list(shape), F32, kind="ExternalInput").ap()

    def dout(name, shape):
        return nc.dram_tensor(name, list(shape), F32, kind="ExternalOutput").ap()

    xp = din("xp", [tp, D])
    xs = din("xs", [NS, TS, D])
    st_a = din("st_a", [DEPTH, NS, 2, D_A])
    st_b = din("st_b", [DEPTH, NS, 30, D_B])
    st_q = din("st_q", [DEPTH, NS, 3, 3 * D_C])
    st_s = din("st_s", [DEPTH, NS, H, DK, DK])
    c_all = din("c_all", [NSTREAM, D])
    norm1_g = din("norm1_g", [DEPTH, D])
    w_ada = din("w_ada", [DEPTH, D, 6 * D])
    b_ada = din("b_ada", [DEPTH, 6 * D])
    w_in = din("w_in", [DEPTH, D, D_IN])
    conv_a_w = din("conv_a_w", [DEPTH, 3, D_A])
    conv_b_w = din("conv_b_w", [DEPTH, 31, D_B])
    conv_b_b = din("conv_b_b", [DEPTH, D_B])
    ln_b_g = din("ln_b_g", [DEPTH, D_B])
    ln_b_b = din("ln_b_b", [DEPTH, D_B])
    conv_qkv_w = din("conv_qkv_w", [DEPTH, 4, 3 * D_C])
    a_log = din("a_log", [DEPTH, H])
    dt_bias = din("dt_bias", [DEPTH, H])
    gdn_norm_g = din("gdn_norm_g", [DEPTH, DK])
    w_out = din("w_out", [DEPTH, D, D])
    norm2_g = din("norm2_g", [DEPTH, D])
    w_gu = din("w_gate_up", [DEPTH, D, 2 * D_FF])
    w_dn = din("w_down", [DEPTH, D_FF, D])
    final_g = din("final_norm_g", [1, D])

    y_p = dout("y_p", [tp, D])
    y_s = dout("y_s", [NS, TS, D])
    na_p = dout("na_p", [DEPTH, 2, D_A])
    nb_p = dout("nb_p", [DEPTH, 30, D_B])
    nq_p = dout("nq_p", [DEPTH, 3, 3 * D_C])
    ns_p = dout("ns_p", [DEPTH, H, DK, DK])
    na_s = dout("na_s", [DEPTH, NS, 2, D_A])
    nb_s = dout("nb_s", [DEPTH, NS, 30, D_B])
    nq_s = dout("nq_s", [DEPTH, NS, 3, 3 * D_C])
    ns_s = dout("ns_s", [DEPTH, NS, H, DK, DK])

    def dscr(name, shape):
        return nc.dram_tensor(name, list(shape), BF16).ap()

    wb_ada = dscr("wb_ada", [DEPTH, D, 6 * D])
    wb_in = dscr("wb_in", [DEPTH, D, D_IN])
    wb_out = dscr("wb_out", [DEPTH, D, D])
    wb_gu = dscr("wb_gu", [DEPTH, D, 2 * D_FF])
    wb_dn = dscr("wb_dn", [DEPTH, D_FF, D])
    wb_cv = dscr("wb_cv", [DEPTH, 128, 128, 128])

    def sb(name, shape, dt=F32):
        return Buf(es.enter_context(nc.sbuf_tensor(name, list(shape), dt)))

    def ps(name, shape, dt=F32):
        return Buf(es.enter_context(nc.psum_tensor(name, list(shape), dt)))

    NW = 4
    wslot = [sb("wslot%d" % i, [128, 8, 512], BF16) for i in range(NW)]
    ident_f = sb("ident_f", [128, 128])
    ident_b = sb("ident_b", [128, 128], BF16)
    negmask = sb("negmask", [128, 128])
    onecol4 = sb("onecol4", [128, 4, 4], BF16)
    onehot4 = sb("onehot4", [4, 4, 128])
    ones_row = sb("ones_row", [1, 128])
    mean1024 = sb("mean1024", [128, 1], BF16)
    mean256 = sb("mean256", [128, 1], BF16)
    stage = sb("stage", [128, 128])

    pcols = sb("pcols", [128, 304])
    C_N1G, C_N2G, C_FNG, C_CBB, C_LNG, C_LNB, C_GNG, C_CAW, C_CBW, C_CQW = 0, 16, 32, 40, 44, 48, 52, 54, 66, 190
    C_LNGH, C_LNBH = 288, 292
    bada = sb("bada", [128, 96])
    hcols = sb("hcols", [4, 8])
    cmod_b = sb("cmod_b", [128, KC, 8], BF16)
    cfm = sb("cfm", [128, KC, 8])
    mcol = sb("mcol", [128, DEPTH, 6, KC, 8])
    halfmc = sb("halfmc", [128, 4])

    NMAX = 512
    xres = sb("xres", [128, KC, NMAX])
    xtok = sb("xtok", [128, 2, D])
    hbuf = sb("hbuf", [128, KC, NMAX], BF16)
    tmpf = [sb("tmpf%d" % i, [128, NMAX]) for i in range(3)]
    zA = sb("zA", [128, 6, NMAX])
    ua = sb("ua", [128, 2, 2 + NMAX], BF16)
    gb = sb("gb", [128, 2, 30 + NMAX], BF16)
    QBW = 3 + NMAX
    QKO = 13 * NMAX
    arena = sb("arena", [128, QKO + 12 * NMAX], BF16)

    class _V:
        pass
    qb = _V(); qb.t = arena.t[:, 0:12 * QBW].rearrange("p (c n) -> p c n", n=QBW); qb.k = arena.k
    qkvs = _V(); qkvs.t = arena.t[:, QKO:QKO + 12 * NMAX].rearrange("p (c n) -> p c n", n=NMAX); qkvs.k = arena.k
    ffa = _V(); ffa.t = arena.t[:, 0:FC * NMAX].rearrange("p (c n) -> p c n", n=NMAX); ffa.k = arena.k
    modt = _V(); modt.t = zA.t[:, 0:2, 0:384].rearrange("p l (c s) -> p l c s", s=8); modt.k = zA.k
    hist_a = sb("hist_a", [128, DEPTH, NS, 2, 2], BF16)
    hist_b = sb("hist_b", [128, DEPTH, NS, 2, 30], BF16)
    hist_q = sb("hist_q", [128, DEPTH, NS, 12, 3], BF16)
    tail_a = sb("tail_a", [128, DEPTH, NS, 2, 2])
    tail_b = sb("tail_b", [128, DEPTH, NS, 30, 2])
    tail_q = sb("tail_q", [128, DEPTH, NS, 3, 12])
    qn = sb("qn", [128, H, NMAX], BF16)
    qe = sb("qe", [128, H, NMAX], BF16)
    kn = sb("kn", [128, H, NMAX], BF16)
    zg = sb("zg", [128, H, NMAX], BF16)
    ymix = sb("ymix", [128, KC, NMAX], BF16)
    rows = {n: sb("r_" + n, [4, NMAX]) for n in
            ("t0", "g", "gam", "ngam", "c3", "egam", "c1", "c4", "rq", "rqe", "rk", "t1")}
    egl_r = sb("egl_r", [4, 4])
    egl_c = sb("egl_c", [128, H, 4])
    cols = [sb("cols%d" % i, [128, 16]) for i in range(4)]
    S_f = sb("S_f", [128, DEPTH, H, DK])
    S_b = sb("S_b", [128, DEPTH, H, DK], BF16)
    def g4(name, dt=BF16):
        return sb("g4_" + name, [128, H, 128], dt)
    G_D = g4("D", F32)
    G_Dn = g4("Dn", F32)
    G_X0, G_Y0 = g4("X0"), g4("Y0")
    G_X = [g4("Xa"), g4("Xb")]
    G_Y = [g4("Ya"), g4("Yb")]
    G_Q = [g4("Qa"), g4("Qb")]
    G_Xo = [g4("Xo%d" % j) for j in range(3)]
    G_T, G_V, G_P, G_PT = g4("T"), g4("V"), g4("P"), g4("PT")
    G_kw, G_kh, G_vb, G_nw, G_vn, G_on = g4("kw"), g4("kh"), g4("vb"), g4("nw"), g4("vn"), g4("on")
    G_sc = sb("g4_sc", [128, 16])
    bmask = [sb("bmask%d" % i, [128, 128], BF16) for i in range(4)]
    esel = sb("esel", [8, 128])
    smask = sb("smask", [128, 128], BF16)

    def psb(name, shape, dt=F32):
        return BankBuf(es.enter_context(nc.psum_tensor(name, list(shape), dt)))

    pm = [psb("pm%d" % i, [128, 512]) for i in range(3)]
    pr = [psb("pr0", [128, 512])]
    ph = [psb("ph%d" % i, [128, 4, 128]) for i in range(4)]
    phb = [ph[i].t[:].bitcast(BF16) for i in range(4)]
    hrings = [Ring(list(range(4))) for _ in range(4)]
    pm_ring = Ring(pm)
    pr_ring = Ring(pr)
    pg_ring = Ring([(ph[i], j) for j in range(4) for i in range(4)])
    tmpf_ring = Ring(tmpf)

    def ACT(out, in_, func, reads, writes, bias=0.0, scale=1.0, accum=None):
        if accum is None:
            return S.op("act", lambda e: e.activation(out=out, in_=in_, func=func, bias=bias, scale=scale), reads, writes)
        return S.op("act", lambda e: e.activation(out=out, in_=in_, func=func, bias=bias, scale=scale,
                                                  accum_out=accum), reads, writes)

    def TT(eng, out, a, b, op, reads, writes):
        return S.op(eng, lambda e: e.tensor_tensor(out=out, in0=a, in1=b, op=op), reads, writes)

    def TSC(eng, out, a, s1, s2, op0, op1, reads, writes):
        if s2 is None:
            return S.op(eng, lambda e: e.tensor_scalar(out=out, in0=a, scalar1=s1, scalar2=None, op0=op0), reads, writes)
        return S.op(eng, lambda e: e.tensor_scalar(out=out, in0=a, scalar1=s1, scalar2=s2, op0=op0, op1=op1), reads, writes)

    def STT(eng, out, in0, scalar, in1, op0, op1, reads, writes):
        return S.op(eng, lambda e: e.scalar_tensor_tensor(out=out, in0=in0, scalar=scalar, in1=in1, op0=op0, op1=op1),
                    reads, writes)

    def CP(eng, out, in_, reads, writes):
        if eng == "act":
            return S.op("act", lambda e: e.copy(out=out, in_=in_), reads, writes)
        return S.op(eng, lambda e: e.tensor_copy(out=out, in_=in_), reads, writes)

    def MSET(eng, ap, val, writes):
        return S.op(eng, lambda e: e.memset(ap, val), (), writes)

    def RSQ(out_ap, out_trk, in_ap, in_trks, scale=1.0, bias=0.0):
        ACT(out_ap, in_ap, AF.Ln, list(in_trks), [out_trk], bias=bias, scale=scale)
        ACT(out_ap, out_ap, AF.Exp, [out_trk], [out_trk], scale=-0.5)

    def TR(out_ap, in_ap, ident_ap, reads, writes):
        return S.op("pe", lambda e: e.transpose(out_ap, in_ap, ident_ap), reads, writes)

    k0 = lambda b: [b.k()]
    MSET("pool", ident_f.t[:], 1.0, k0(ident_f))
    S.op("pool", lambda e: e.affine_select(out=ident_f.t[:], in_=ident_f.t[:], pattern=[[-1, 128]],
                                           compare_op=ALU.is_equal, fill=0.0, base=0, channel_multiplier=1),
         k0(ident_f), k0(ident_f))
    CP("pool", ident_b.t[:], ident_f.t[:], k0(ident_f), k0(ident_b))
    MSET("pool", negmask.t[:], -30000.0, k0(negmask))
    S.op("pool", lambda e: e.affine_select(out=negmask.t[:], in_=negmask.t[:], pattern=[[1, 128]],
                                           compare_op=ALU.is_gt, fill=0.0, base=0, channel_multiplier=-1),
         k0(negmask), k0(negmask))
    smask_f = tmpf[2]
    MSET("pool", smask_f.t[:, 0:128], 1.0, k0(smask_f))
    S.op("pool", lambda e: e.affine_select(out=smask_f.t[:, 0:128], in_=smask_f.t[:, 0:128], pattern=[[-1, 128]],
                                           compare_op=ALU.is_gt, fill=0.0, base=0, channel_multiplier=1),
         k0(smask_f), k0(smask_f))
    CP("pool", smask.t[:], smask_f.t[:, 0:128], k0(smask_f), k0(smask))
    MSET("dve", onecol4.t[:], 0.0, k0(onecol4))
    for h in range(H):
        MSET("dve", onecol4.t[:, h, h:h + 1], 1.0, k0(onecol4))
    MSET("pool", onehot4.t[:], 1.0, k0(onehot4))
    for h in range(H):
        S.op("pool", lambda e, h=h: e.affine_select(out=onehot4.t[:, h, :], in_=onehot4.t[:, h, :], pattern=[[0, 128]],
                                                    compare_op=ALU.is_equal, fill=0.0, base=-h, channel_multiplier=1),
             k0(onehot4), k0(onehot4))
    MSET("dve", ones_row.t[:], 1.0, k0(ones_row))
    MSET("dve", mean1024.t[:], 1.0 / 1024.0, k0(mean1024))
    MSET("dve", mean256.t[:], 1.0 / 256.0, k0(mean256))
    MSET("dve", halfmc.t[:], 0.5, k0(halfmc))

    class _BV:
        def __init__(self, b):
            self.t = b.t[:, 0:128]
            self.k = b.k
    bdt = [_BV(tmpf[0]), _BV(tmpf[1])]
    prev = None
    for mi, bs_ in enumerate((16, 32, 64)):
        ng = 128 // bs_
        MSET("pool", esel.t[0:ng, :], 1.0, k0(esel))
        S.op("pool", lambda e, ng=ng, bs_=bs_: e.affine_select(out=esel.t[0:ng, :], in_=esel.t[0:ng, :], pattern=[[1, 128]],
                                                               compare_op=ALU.is_ge, fill=0.0, base=0, channel_multiplier=-bs_),
             k0(esel), k0(esel))
        S.op("pool", lambda e, ng=ng, bs_=bs_: e.affine_select(out=esel.t[0:ng, :], in_=esel.t[0:ng, :], pattern=[[-1, 128]],
                                                               compare_op=ALU.is_gt, fill=0.0, base=bs_, channel_multiplier=bs_),
             k0(esel), k0(esel))
        pbk = pm_ring.next()
        S.mm(pbk.t[:, 0:128], [(esel.t[0:ng, :], esel.t[0:ng, :])], k0(esel), [pbk.k()])
        cur_ = bdt[mi % 2]
        CP("dve", cur_.t[:], pbk.t[:, 0:128], [pbk.k()], k0(cur_))
        if prev is None:
            CP("dve", bmask[0].t[:], cur_.t[:], k0(cur_), k0(bmask[0]))
        else:
            TT("dve", bmask[mi].t[:], cur_.t[:], prev.t[:], ALU.subtract, k0(cur_) + k0(prev), k0(bmask[mi]))
        prev = cur_
    TSC("dve", bmask[3].t[:], prev.t[:], -1.0, 1.0, ALU.mult, ALU.add, k0(prev), k0(bmask[3]))

    wtrk = {}

    def conv_w(name, src, dst, rows):
        for l in range(DEPTH):
            for r0 in range(0, rows, 128):
                t = wtrk[(name, l, r0 // 128)] = Trk()
                S.dma(dst[l, r0:r0 + 128, :], src[l, r0:r0 + 128, :], (), [t], eng="pool")

    conv_w("ada", w_ada, wb_ada, D)
    conv_w("in", w_in, wb_in, D)
    conv_w("out", w_out, wb_out, D)
    conv_w("gu", w_gu, wb_gu, D)
    conv_w("dn", w_dn, wb_dn, D_FF)

    def load_rows_T(dst_ap, src_ap, R, dst_trk, evac="dve"):
        S.dma(stage.t[0:R, :], src_ap, (), k0(stage))
        pb, j = pg_ring.next()
        TR(pb.t[:, j, 0:R], stage.t[0:R, :], ident_f.t[0:R, 0:R], k0(stage) + k0(ident_f), [pb.k(j)])
        CP(evac, dst_ap, pb.t[:, j, 0:R], [pb.k(j)], [dst_trk])

    pk = pcols.k()
    load_rows_T(pcols.t[:, C_N1G:C_N1G + 16], norm1_g.rearrange("l (c p) -> (l c) p", p=128), 16, pk)
    load_rows_T(pcols.t[:, C_N2G:C_N2G + 16], norm2_g.rearrange("l (c p) -> (l c) p", p=128), 16, pk)
    load_rows_T(pcols.t[:, C_FNG:C_FNG + 8], final_g.rearrange("l (c p) -> (l c) p", p=128), 8, pk)
    load_rows_T(pcols.t[:, C_CBB:C_CBB + 4], conv_b_b.rearrange("l (c p) -> (l c) p", p=128), 4, pk)
    load_rows_T(pcols.t[:, C_LNG:C_LNG + 4], ln_b_g.rearrange("l (c p) -> (l c) p", p=128), 4, pk)
    load_rows_T(pcols.t[:, C_LNB:C_LNB + 4], ln_b_b.rearrange("l (c p) -> (l c) p", p=128), 4, pk)
    load_rows_T(pcols.t[:, C_GNG:C_GNG + 2], gdn_norm_g, 2, pk)
    load_rows_T(pcols.t[:, C_CAW:C_CAW + 12], conv_a_w.rearrange("l t (c p) -> (l t c) p", p=128), 12, pk)
    load_rows_T(pcols.t[:, C_CBW:C_CBW + 124], conv_b_w.rearrange("l t (c p) -> (l t c) p", p=128), 124, pk)
    load_rows_T(pcols.t[:, C_CQW:C_CQW + 96], conv_qkv_w.rearrange("l t (c p) -> (l t c) p", p=128), 96, pk)
    load_rows_T(bada.t[:, 0:96], b_ada.rearrange("l (c p) -> (l c) p", p=128), 96, bada.k())
    TSC("dve", pcols.t[:, C_LNGH:C_LNGH + 4], pcols.t[:, C_LNG:C_LNG + 4], 0.5, None, ALU.mult, None, [pk], [pk])
    TSC("dve", pcols.t[:, C_LNBH:C_LNBH + 4], pcols.t[:, C_LNB:C_LNB + 4], 0.5, None, ALU.mult, None, [pk], [pk])
    TSC("dve", pcols.t[:, C_GNG:C_GNG + 2], pcols.t[:, C_GNG:C_GNG + 2], 0.5, None, ALU.mult, None, [pk], [pk])

    with nc.allow_non_contiguous_dma(reason="tiny per-head params"):
        S.dma(hcols.t[:, 0:2], dt_bias.rearrange("l h -> h l"), (), k0(hcols))
        S.dma(hcols.t[:, 4:6], a_log.rearrange("l h -> h l"), (), k0(hcols))
    ACT(hcols.t[:, 2:4], hcols.t[:, 4:6], AF.Exp, k0(hcols), k0(hcols))
    TSC("dve", hcols.t[:, 2:4], hcols.t[:, 2:4], -1.0, None, ALU.mult, None, k0(hcols), k0(hcols))

    for c in range(KC):
        load_rows_T(cfm.t[:, c, 0:nstream], c_all[0:nstream, c * 128:(c + 1) * 128], nstream, cfm.k())
    tf = tmpf_ring.next()
    cfv = cfm.t[:, :, 0:nstream]
    tfv = tf.t[:, 0:KC * nstream].rearrange("p (c s) -> p c s", s=nstream)
    ACT(tfv, cfv, AF.Tanh, k0(cfm), k0(tf), scale=0.5)
    STT("dve", tfv, tfv, 1.0, cfv, ALU.add, ALU.mult, k0(tf) + k0(cfm), k0(tf))
    TSC("dve", cmod_b.t[:, :, 0:nstream], tfv, 0.5, None, ALU.mult, None, k0(tf), k0(cmod_b))

    pieces = []

    def add_piece(out_fn, in_ap, rtrk):
        pieces.append([(out_fn, in_ap, rtrk, (0, 1))])
        return len(pieces) - 1

    def add_piece2(subs):
        pieces.append(subs)
        return len(pieces) - 1

    def wk(slot):
        return [slot.k(0), slot.k(1)]

    class WS:
        issued = 0

    def w_issue_upto(idx):
        while WS.issued <= idx and WS.issued < len(pieces):
            slot = wslot[WS.issued % NW]
            for out_fn, in_ap, rtrk, keys in pieces[WS.issued]:
                S.dma(out_fn(slot.t), in_ap, rtrk, [slot.k(x) for x in keys])
            WS.issued += 1

    def w_get(idx, look=NW - 1):
        w_issue_upto(idx + look)
        return wslot[idx % NW]

    def kp(ap2d):
        return ap2d.rearrange("(k p) n -> p k n", p=128)

    ada_trk = lambda l: [wtrk[("ada", l, r)] for r in range(KC)]
    in_trk = lambda l: [wtrk[("in", l, r)] for r in range(KC)]
    out_trk = lambda l: [wtrk[("out", l, r)] for r in range(KC)]
    gu_trk = lambda l: [wtrk[("gu", l, r)] for r in range(KC)]
    cvtrk = [[Trk() for _ in range(4)] for _ in range(DEPTH)]

    ada_pieces = {}
    for l in range(DEPTH):
        for j in range(12):
            ada_pieces[(l, j)] = add_piece(lambda t: t[:, :, :], kp(wb_ada[l][:, j * 512:(j + 1) * 512]), ada_trk(l))
    for l in range(DEPTH):
        pmod = pm_ring.next()
        pmv = pmod.t[:, 0:48 * 8].rearrange("p (c s) -> p c s", s=8)
        for j in range(12):
            slot = w_get(ada_pieces[(l, j)])
            for m in range(4):
                oc = j * 4 + m
                S.mm(pmv[:, oc, 0:nstream],
                     [(slot.t[:, kc, m * 128:(m + 1) * 128], cmod_b.t[:, kc, 0:nstream]) for kc in range(KC)],
                     wk(slot) + k0(cmod_b), [pmod.k()])
        for s in range(nstream):
            TT("dve", modt.t[:, l, :, s], pmv[:, :, s], bada.t[:, l * 48:(l + 1) * 48], ALU.add,
               [pmod.k(), bada.k()], k0(modt))
        for s in range(nstream):
            for which, (sc0, sh0, g0, gcol) in enumerate(((8, 0, 16, C_N1G), (32, 24, 40, C_N2G))):
                base = which * 3
                STT("dve", mcol.t[:, l, base + 0, :, s], modt.t[:, l, sc0:sc0 + 8, s], 1.0,
                    pcols.t[:, gcol + l * 8:gcol + l * 8 + 8], ALU.add, ALU.mult, k0(modt) + [pk], k0(mcol))
                CP("dve", mcol.t[:, l, base + 1, :, s], modt.t[:, l, sh0:sh0 + 8, s], k0(modt), k0(mcol))
                CP("dve", mcol.t[:, l, base + 2, :, s], modt.t[:, l, g0:g0 + 8, s], k0(modt), k0(mcol))

    def cv_col(l, kind, c, t):
        if kind == "a":
            return C_CAW + (l * 3 + t) * 2 + c
        if kind == "b":
            return C_CBW + (l * 31 + t) * 2 + c
        return C_CQW + (l * 4 + t) * 12 + c

    for l in range(DEPTH):
        mats = [("a", c, t) for c in range(2) for t in range(3)] + \
               [("b", c, t) for c in range(2) for t in range(31)] + \
               [("q", c, t) for c in range(12) for t in range(4)]
        for g0 in range(0, len(mats), 32):
            grp = mats[g0:g0 + 32]
            slot = wslot[(g0 // 32) % NW]
            for i, (kind, c, t) in enumerate(grp):
                col = cv_col(l, kind, c, t)
                TSC("dve", slot.t[:, i // 4, (i % 4) * 128:(i % 4 + 1) * 128], ident_f.t[:], pcols.t[:, col:col + 1],
                    None, ALU.mult, None, k0(ident_f) + [pk], k0(slot))
            n = len(grp)
            slotv = slot.t[:].rearrange("p a (b c) -> p (a b) c", c=128)
            S.dma(wb_cv[l, g0:g0 + n].rearrange("m p c -> p m c"), slotv[:, 0:n, :], k0(slot), [cvtrk[l][g0 // 32]])

    if dbg == "stage0":
        d_mod = dout("d_mod", [128, DEPTH * 48 * 8])
        d_mcol = dout("d_mcol", [128, DEPTH * 6 * KC * 8])
        d_pcols = dout("d_pcols", [128, 304])
        for l in range(DEPTH):
            S.dma(d_mod[:, l * 384:(l + 1) * 384], zA.t[:, l, 0:384], k0(modt), ())
        S.dma(d_mcol, mcol.t[:].rearrange("p l a c s -> p (l a c s)"), k0(mcol), ())
        S.dma(d_pcols, pcols.t[:], [pk], ())
        S.finish()
        return

    ones4 = sb("ones4", [4, 128])
    MSET("dve", ones4.t[:], 1.0, k0(ones4))

    tiles = []
    if ns > 0:
        tiles.append(("S", 0, ns * TS, True, True))
    npt = tp // 512
    for i in range(npt):
        tiles.append((0, i * 512, 512, i == 0, i == npt - 1))

    def cv_out(n):
        return lambda t: t[:].rearrange("p a (b c) -> p (a b) c", c=128)[:, 0:n, :]

    def layer_pieces(l):
        P = {"cv": [], "in": [], "out": [[], []], "gu": [[], []], "dn": [[], []]}
        for j in range(7):
            ncol = min(512, D_IN - 512 * j)
            P["in"].append(add_piece(lambda t, ncol=ncol: t[:, :, 0:ncol], kp(wb_in[l][:, j * 512:j * 512 + ncol]), in_trk(l)))
        for j in range(4):
            n = min(32, NCV - 32 * j)
            P["cv"].append(add_piece(cv_out(n), wb_cv[l, 32 * j:32 * j + n].rearrange("m p c -> p m c"), cvtrk[l]))
        for hf in range(2):
            for j in range(2):
                P["out"][hf].append(add_piece(lambda t: t[:, :, :], kp(wb_out[l][:, j * 512:(j + 1) * 512]), out_trk(l)))
            for j in range(11):
                P["gu"][hf].append(add_piece2([
                    (lambda t: t[:, :, 0:256], kp(wb_gu[l][:, j * 256:(j + 1) * 256]), gu_trk(l), (0,)),
                    (lambda t: t[:, :, 256:512], kp(wb_gu[l][:, D_FF + j * 256:D_FF + (j + 1) * 256]), gu_trk(l), (1,))]))
            for j in range(8):
                P["dn"][hf].append(add_piece(
                    lambda t: t[:].rearrange("p a b -> p (a b)")[:, 0:FC * 128].rearrange("p (k c) -> p k c", c=128),
                    wb_dn[l][:, j * 128:(j + 1) * 128].rearrange("(k p) n -> p k n", p=128),
                    [wtrk[("dn", l, r)] for r in range(FC)]))
        return P

    tile_P = [[layer_pieces(l) for l in range(DEPTH)] for _ in tiles]

    alt = Ring(["act", "dve"])
    XK = lambda cs: xres.ks(cs)

    def rms_row(c0, c1, src_keys_fn, sq_src, nchunks, meanvec, eps, tag):
        for c in range(nchunks):
            ACT(ymix.t[:, c, c0:c1], sq_src(c), AF.Square, [src_keys_fn(c)], [ymix.k(c)])
        prow = pr_ring.next()
        S.mm(prow.t[0:1, c0:c1], [(meanvec.t[:, 0:1], ymix.t[:, c, c0:c1]) for c in range(nchunks)],
             ymix.ks(range(nchunks)) + k0(meanvec), [prow.k()])
        r0 = rows["t0"]
        RSQ(r0.t[0:1, c0:c1], r0.k(), prow.t[0:1, c0:c1], [prow.k()], bias=eps)
        pb = pm_ring.next()
        S.mm(pb.t[:, c0:c1], [(ones_row.t[0:1, :], r0.t[0:1, c0:c1])], [r0.k(), ones_row.k()], [pb.k()])
        return pb

    def norm_mod(c0, c1, l, which, segs):
        pb = rms_row(c0, c1, lambda c: xres.k(c), lambda c: xres.t[:, c, c0:c1], KC, mean1024, EPS, "n")
        for c in range(KC):
            tf = tmpf_ring.next()
            TT("dve", tf.t[:, c0:c1], xres.t[:, c, c0:c1], pb.t[:, c0:c1], ALU.mult, [xres.k(c), pb.k()], [tf.k()])
            for sidx, s0, n in segs:
                ACT(hbuf.t[:, c, s0:s0 + n], tf.t[:, s0:s0 + n], AF.Identity, [tf.k(), mcol.k()], [hbuf.k(c)],
                    bias=mcol.t[:, l, which * 3 + 1, c, sidx:sidx + 1], scale=mcol.t[:, l, which * 3 + 0, c, sidx:sidx + 1])

    def load_x(x_src, N):
        rb = min(128, N)
        for half in range(0, N, 256):
            nb = min(256, N - half)
            nblk = (nb + 127) // 128
            for b in range(nblk):
                S.dma(xtok.t[0:rb, b, :], x_src[half + b * 128:half + b * 128 + rb, :], (), [xtok.k(b)])
            for c in range(KC):
                p = pm_ring.next()
                for b in range(nblk):
                    TR(p.t[:, b * 128:b * 128 + rb], xtok.t[0:rb, b, c * 128:(c + 1) * 128], ident_f.t[0:rb, 0:rb],
                       [xtok.k(b), ident_f.k()], [p.k()])
                CP(alt.next(), xres.t[:, c, half:half + nb], p.t[:, 0:nb], [p.k()], [xres.k(c)])

    def store_y(y_dst, N):
        rb = min(128, N)
        for half in range(0, N, 256):
            nb = min(256, N - half)
            nblk = (nb + 127) // 128
            for b in range(nblk):
                t0_ = half + b * 128
                for cg in range(2):
                    p = pm_ring.next()
                    for c4 in range(4):
                        c = cg * 4 + c4
                        TR(p.t[0:rb, c4 * 128:(c4 + 1) * 128], xres.t[:, c, t0_:t0_ + rb], ident_f.t[:, :],
                           [xres.k(c), ident_f.k()], [p.k()])
                    CP(alt.next(), xtok.t[0:rb, b, cg * 512:(cg + 1) * 512], p.t[0:rb, 0:512], [p.k()], [xtok.k(b)])
                S.dma(y_dst[t0_:t0_ + rb, :], xtok.t[0:rb, b, :], [xtok.k(b)], ())

    def store_rows_T(dst_ap, src_ap, R, src_trk):
        pb, j = pg_ring.next()
        TR(pb.t[0:R, j, :], src_ap, ident_f.t[:, :], [src_trk, ident_f.k()], [pb.k(j)])
        CP("dve", stage.t[0:R, :], pb.t[0:R, j, :], [pb.k(j)], k0(stage))
        S.dma(dst_ap, stage.t[0:R, :], k0(stage), ())

    def run_gens(gens):
        gens = list(gens)
        while gens:
            for g in list(gens):
                try:
                    next(g)
                except StopIteration:
                    gens.remove(g)

    def layer(ti, l, stream, N, first, last):
        CKL[0] = l
        P = tile_P[ti][l]
        sample = (stream == "S")
        nseg = ns if sample else 1
        ns_ = N // nseg
        SEG = [(1 + q_, q_ * ns_, ns_) for q_ in range(nseg)] if sample else [(0, 0, N)]
        L = TS if sample else min(128, N)
        nblk = N // L

        def cin(buf, c, hw):
            return buf.t[:, c, 0:nseg * (hw + ns_)].rearrange("p (s w) -> p s w", w=hw + ns_)

        def cin_all(buf, hw):
            return buf.t[:, :, 0:nseg * (hw + ns_)].rearrange("p c (s w) -> p c s w", w=hw + ns_)

        def pv(ap):
            return ap.rearrange("p (s n) -> p s n", n=ns_)

        CP("dve", cin_all(ua, 2)[:, :, :, 0:2], hist_a.t[:, l, 0:nseg, :, :].rearrange("p s c t -> p c s t"), k0(hist_a), k0(ua))
        CP("dve", cin_all(gb, 30)[:, :, :, 0:30], hist_b.t[:, l, 0:nseg, :, :].rearrange("p s c t -> p c s t"), k0(hist_b), k0(gb))
        CP("dve", cin_all(qb, 3)[:, :, :, 0:3], hist_q.t[:, l, 0:nseg, :, :].rearrange("p s c t -> p c s t"), k0(hist_q), [qb.k("h")])

        ck(1)
        norm_mod(0, N, l, 0, SEG)
        ck(2)

        HK = hbuf.ks(range(KC))
        for j in range(7):
            slot = w_get(P["in"][j])
            nfull = min(4, 26 - 4 * j)
            for m in range(nfull):
                oc = j * 4 + m
                p = pm_ring.next()
                S.mm(p.t[:, 0:N], [(slot.t[:, kc, m * 128:(m + 1) * 128], hbuf.t[:, kc, 0:N]) for kc in range(KC)],
                     wk(slot) + HK, [p.k()])
                pN = p.t[:, 0:N]
                if oc < 4:
                    CP("act", zA.t[:, oc, 0:N], pN, [p.k()], [zA.k(oc)])
                elif oc < 6:
                    c = oc - 4
                    TT("dve", cin(ua, c, 2)[:, :, 2:2 + ns_], pv(p.t[:, 0:N]), pv(zA.t[:, c, 0:N]), ALU.mult, [p.k(), zA.k(c)], k0(ua))
                    if last:
                        for si, (_, c0, n) in enumerate(SEG):
                            TT("dve", tail_a.t[:, l, si, :, c], p.t[:, c0 + n - 2:c0 + n], zA.t[:, c, c0 + n - 2:c0 + n], ALU.mult,
                               [p.k(), zA.k(c)], k0(tail_a))
                elif oc < 8:
                    CP("act", zA.t[:, oc - 2, 0:N], pN, [p.k()], [zA.k(oc - 2)])
                elif oc < 10:
                    c = oc - 8
                    tf = tmpf_ring.next()
                    ACT(tf.t[:, 0:N], pN, AF.Tanh, [p.k()], [tf.k()], scale=0.5)
                    STT("dve", tf.t[:, 0:N], tf.t[:, 0:N], 1.0, zA.t[:, 4 + c, 0:N], ALU.add, ALU.mult,
                        [tf.k(), zA.k(4 + c)], [tf.k()])
                    ACT(cin(gb, c, 30)[:, :, 30:30 + ns_], pv(tf.t[:, 0:N]), AF.Copy, [tf.k()], k0(gb), scale=0.5)
                    if last:
                        for si, (_, c0, n) in enumerate(SEG):
                            TSC("dve", tail_b.t[:, l, si, :, c], tf.t[:, c0 + n - 30:c0 + n], 0.5, None, ALU.mult, None, [tf.k()], k0(tail_b))
                elif oc < 22:
                    c = oc - 10
                    CP(alt.next(), cin(qb, c, 3)[:, :, 3:3 + ns_], pv(pN), [p.k()], [qb.k()])
                    if last:
                        for si, (_, c0, n) in enumerate(SEG):
                            CP("dve", tail_q.t[:, l, si, :, c], p.t[:, c0 + n - 3:c0 + n], [p.k()], k0(tail_q))
                else:
                    hh = oc - 22
                    tf = tmpf_ring.next()
                    ACT(tf.t[:, 0:N], pN, AF.Tanh, [p.k()], [tf.k()], scale=0.5)
                    STT("dve", zg.t[:, hh, 0:N], tf.t[:, 0:N], 1.0, pN, ALU.add, ALU.mult, [tf.k(), p.k()], [zg.k(hh)])
            if j == 6:
                R = rows
                pa = pr_ring.next()
                S.mm(pa.t[0:4, 0:N], [(slot.t[:, kc, 256:260], hbuf.t[:, kc, 0:N]) for kc in range(KC)], wk(slot) + HK, [pa.k()])
                ACT(R["t0"].t[0:4, 0:N], pa.t[0:4, 0:N], AF.Exp, [pa.k(), hcols.k()], [R["t0"].k()], bias=hcols.t[:, l:l + 1])
                pbb = pr_ring.next()
                S.mm(pbb.t[0:4, 0:N], [(slot.t[:, kc, 260:264], hbuf.t[:, kc, 0:N]) for kc in range(KC)], wk(slot) + HK, [pbb.k()])
                ACT(R["c3"].t[0:4, 0:N], pbb.t[0:4, 0:N], AF.Tanh, [pbb.k()], [R["c3"].k()], scale=0.5)

        R = rows
        RK = lambda n: R[n].k()
        rv = lambda n: R[n].t[0:4, 0:N]
        ACT(rv("t0"), rv("t0"), AF.Ln, [RK("t0")], [RK("t0")], bias=1.0)
        TSC("dve", rv("g"), rv("t0"), hcols.t[:, 2 + l:3 + l], None, ALU.mult, None, [RK("t0"), hcols.k()], [RK("g")])
        TSC("dve", rv("c3"), rv("c3"), 0.5, 0.5, ALU.mult, ALU.add, [RK("c3")], [RK("c3")])
        ck(41)
        for b in range(nblk):
            bs = slice(b * L, (b + 1) * L)
            S.op("dve", lambda e, bs=bs: e.tensor_tensor_scan(out=R["gam"].t[0:4, bs], data0=ones4.t[0:4, 0:L],
                                                               data1=R["g"].t[0:4, bs], initial=0.0,
                                                               op0=ALU.mult, op1=ALU.add),
                 [RK("g"), ones4.k()], [RK("gam")])
        ck(42)
        TSC("dve", rv("ngam"), rv("gam"), -1.0, None, ALU.mult, None, [RK("gam")], [RK("ngam")])
        ACT(rv("egam"), rv("gam"), AF.Exp, [RK("gam")], [RK("egam")])
        TT("dve", rv("c1"), rv("c3"), rv("egam"), ALU.mult, [RK("c3"), RK("egam")], [RK("c1")])
        TSC("dve", rv("c4"), rv("c3"), -1.0, None, ALU.mult, None, [RK("c3")], [RK("c4")])
        TSC("dve", rv("c3"), rv("c3"), 0.5, None, ALU.mult, None, [RK("c3"), RK("c1"), RK("c4")], [RK("c3")])
        for b in range(nblk):
            bs = slice(b * L, (b + 1) * L)
            e_ = (b + 1) * L - 1
            ACT(R["g"].t[0:4, bs], R["gam"].t[0:4, bs], AF.Exp, [RK("gam")], [RK("g")],
                bias=R["gam"].t[0:4, e_:e_ + 1], scale=-1.0)
            CP("dve", egl_r.t[0:4, b:b + 1], R["egam"].t[0:4, e_:e_ + 1], [RK("egam")], k0(egl_r))
        ck(3)
        cvs = [w_get(P["cv"][g], look=3 - g) for g in range(4)]
        cvk = [t_ for x in cvs for t_ in wk(x)]

        def diag(mi):
            return cvs[mi // 32].t[:].rearrange("p a (b c) -> p (a b) c", c=128)[:, mi % 32, :]

        for c in range(2):
            p = pm_ring.next()
            S.mm(pv(p.t[:, 0:N]), [(diag(c * 3 + t), cin(ua, c, 2)[:, :, t:t + ns_]) for t in range(3)], cvk + k0(ua), [p.k()])
            TT("dve", ymix.t[:, c, 0:N], p.t[:, 0:N], zA.t[:, 2 + c, 0:N], ALU.mult, [p.k(), zA.k(2 + c)], [ymix.k(c)])
        if not sample:
            CP("dve", hist_a.t[:, l, 0, :, :], ua.t[:, :, N:N + 2], k0(ua), k0(hist_a))
        for c in range(12):
            p = pm_ring.next()
            S.mm(pv(p.t[:, 0:N]), [(diag(68 + c * 4 + t), cin(qb, c, 3)[:, :, t:t + ns_]) for t in range(4)],
                 cvk + [qb.k(), qb.k("h")], [p.k()])
            tf = tmpf_ring.next()
            ACT(tf.t[:, 0:N], p.t[:, 0:N], AF.Tanh, [p.k()], [tf.k()], scale=0.5)
            STT("dve", qkvs.t[:, c, 0:N], tf.t[:, 0:N], 1.0, p.t[:, 0:N], ALU.add, ALU.mult, [tf.k(), p.k()], [qkvs.k(c)])
        if not sample:
            CP("dve", hist_q.t[:, l, 0, :, :], qb.t[:, :, N:N + 3], [qb.k(), qb.k("h")], k0(hist_q))
        for c in range(2):
            p = pm_ring.next()
            S.mm(pv(p.t[:, 0:N]), [(diag(6 + c * 31 + t), cin(gb, c, 30)[:, :, t:t + ns_]) for t in range(31)], cvk + k0(gb), [p.k()])
            ACT(zA.t[:, 4 + c, 0:N], p.t[:, 0:N], AF.Identity, [p.k(), pk], [zA.k(4 + c)],
                bias=pcols.t[:, C_CBB + l * 2 + c:C_CBB + l * 2 + c + 1])
            CP("dve", ymix.t[:, 4 + c, 0:N], zA.t[:, 4 + c, 0:N], [zA.k(4 + c)], [ymix.k(4 + c)])
        if not sample:
            CP("dve", hist_b.t[:, l, 0, :, :], gb.t[:, :, N:N + 30], k0(gb), k0(hist_b))
        rm, ve = rows["t1"], rows["rk"]
        pr1 = pr_ring.next()
        S.mm(pr1.t[0:1, 0:N], [(mean256.t[:, 0:1], ymix.t[:, 4 + c, 0:N]) for c in range(2)],
             ymix.ks((4, 5)) + k0(mean256), [pr1.k()])
        CP("act", rm.t[0:1, 0:N], pr1.t[0:1, 0:N], [pr1.k()], [rm.k()])
        pbm = pm_ring.next()
        S.mm(pbm.t[:, 0:N], [(ones_row.t[0:1, :], rm.t[0:1, 0:N])], [rm.k(), ones_row.k()], [pbm.k()])
        for c in range(2):
            TT("dve", zA.t[:, 4 + c, 0:N], zA.t[:, 4 + c, 0:N], pbm.t[:, 0:N], ALU.subtract, [zA.k(4 + c), pbm.k()], [zA.k(4 + c)])
            ACT(ymix.t[:, 6 + c, 0:N], zA.t[:, 4 + c, 0:N], AF.Square, [zA.k(4 + c)], [ymix.k(6 + c)])
        pr2 = pr_ring.next()
        S.mm(pr2.t[0:1, 0:N], [(mean256.t[:, 0:1], ymix.t[:, 6 + c, 0:N]) for c in range(2)],
             ymix.ks((6, 7)) + k0(mean256), [pr2.k()])
        RSQ(ve.t[0:1, 0:N], ve.k(), pr2.t[0:1, 0:N], [pr2.k()], bias=EPS)
        pb1 = pm_ring.next()
        S.mm(pb1.t[:, 0:N], [(ones_row.t[0:1, :], ve.t[0:1, 0:N])], [ve.k(), ones_row.k()], [pb1.k()])
        for c in range(2):
            tf = tmpf_ring.next()
            TT("dve", tf.t[:, 0:N], zA.t[:, 4 + c, 0:N], pb1.t[:, 0:N], ALU.mult, [zA.k(4 + c), pb1.k()], [tf.k()])
            lc = l * 2 + c
            tf2 = tmpf_ring.next()
            ACT(tf2.t[:, 0:N], tf.t[:, 0:N], AF.Tanh, [tf.k(), pk], [tf2.k()],
                bias=pcols.t[:, C_LNBH + lc:C_LNBH + lc + 1], scale=pcols.t[:, C_LNGH + lc:C_LNGH + lc + 1])
            ACT(tf.t[:, 0:N], tf.t[:, 0:N], AF.Identity, [tf.k(), pk], [tf.k()],
                bias=pcols.t[:, C_LNBH + lc:C_LNBH + lc + 1], scale=pcols.t[:, C_LNGH + lc:C_LNGH + lc + 1])
            STT("dve", ymix.t[:, 2 + c, 0:N], tf2.t[:, 0:N], 1.0, tf.t[:, 0:N], ALU.add, ALU.mult,
                [tf.k(), tf2.k()], [ymix.k(2 + c)])

        ck(4)
        R = rows
        RK = lambda n: R[n].k()
        rv = lambda n: R[n].t[0:4, 0:N]
        ck(43)
        for base, name in ((0, "rq"), (4, "rk")):
            prn = pr_ring.next()
            for hh in range(H):
                ACT(ymix.t[:, 4 + hh, 0:N], qkvs.t[:, base + hh, 0:N], AF.Square, [qkvs.k(base + hh)], [ymix.k(4 + hh)])
            S.mm(prn.t[0:4, 0:N], [(onecol4.t[:, hh, :], ymix.t[:, 4 + hh, 0:N]) for hh in range(H)],
                 ymix.ks(range(4, 8)) + k0(onecol4), [prn.k()])
            RSQ(rv(name), RK(name), prn.t[0:4, 0:N], [prn.k()], bias=4 * EPS)
        ck(44)
        STT("dve", rv("rqe"), rv("rq"), DK ** -0.5, rv("egam"), ALU.mult, ALU.mult, [RK("rq"), RK("egam")], [RK("rqe")])
        TSC("dve", rv("rq"), rv("rq"), DK ** -0.5, None, ALU.mult, None, [RK("rq")], [RK("rq")])
        for hh in range(H):
            for rname, dst, src in (("rq", qn, hh), ("rqe", qe, hh), ("rk", kn, 4 + hh)):
                pbq = pm_ring.next()
                S.mm(pbq.t[:, 0:N], [(onehot4.t[:, hh, :], rv(rname))], [RK(rname), onehot4.k()], [pbq.k()])
                TT("dve", dst.t[:, hh, 0:N], qkvs.t[:, src, 0:N], pbq.t[:, 0:N], ALU.mult, [qkvs.k(src), pbq.k()], [dst.k(hh)])
            pgb, pj = pg_ring.next()
            S.mm(pgb.t[:, pj, 0:nblk], [(onehot4.t[:, hh, :], egl_r.t[0:4, 0:nblk])], k0(egl_r) + k0(onehot4), [pgb.k(pj)])
            CP("act", egl_c.t[:, hh, 0:nblk], pgb.t[:, pj, 0:nblk], [pgb.k(pj)], [egl_c.k(hh)])
        ck(45)
        for b in range(nblk):
            bs = slice(b * L, (b + 1) * L)
            pgb, pj = pg_ring.next()
            for q_, name in enumerate(("c1", "g", "c3", "c4")):
                TR(pgb.t[0:L, pj, 4 * q_:4 * q_ + 4], R[name].t[0:4, bs], ident_f.t[0:4, 0:4],
                   [RK(name), ident_f.k()], [pgb.k(pj)])
            CP("act", cols[b].t[0:L, 0:16], pgb.t[0:L, pj, 0:16], [pgb.k(pj)], k0(cols[b]))

        ck(5)
        bank_ring = Ring(ph)
        phbv = {id(ph[i]): phb[i] for i in range(4)}
        SKs = [S_b.k((l, hh)) for hh in range(H)]
        SFKs = [S_f.k((l, hh)) for hh in range(H)]
        Sf4 = S_f.t[:, l, :, :]
        Sb4 = S_b.t[:, l, :, :]

        def gdn_chain(b, ci, h0, h1, banks):
            nh = h1 - h0
            c0_ = b * L
            cs = slice(c0_, c0_ + L)
            col = cols[b]
            ck_ = col.k()
            HS = range(h0, h1)
            KK = lambda t4: [t4.k(ci)]
            v3 = lambda t4: t4.t[0:L, h0:h1, 0:L]
            bc_col = lambda q_: col.t[0:L, q_ * 4 + h0:q_ * 4 + h1].unsqueeze(2).to_broadcast([L, nh, 128])
            bc_colL = lambda q_: col.t[0:L, q_ * 4 + h0:q_ * 4 + h1].unsqueeze(2).to_broadcast([L, nh, L])
            bc_m = lambda mk: mk.t[0:L, 0:L].unsqueeze(1).to_broadcast([L, nh, L])
            Ib = ident_b.t[0:L, 0:L]
            If = ident_f.t[0:L, 0:L]
            bring = Ring(banks)

            def pbank():
                bk = bring.next()
                return bk, phbv[id(bk)]

            sks = [SKs[hh] for hh in HS]
            sfks = [SFKs[hh] for hh in HS]
            sf = S_f.t[:, l, h0:h1, :]
            sbv = S_b.t[:, l, h0:h1, :]
            bA, _ = pbank()
            for hh in HS:
                S.mm(bA.t[0:L, hh, 0:L], [(kn.t[:, hh, cs], kn.t[:, hh, cs])], [kn.k(hh)], [bA.k()])
            bB, _ = pbank()
            for hh in HS:
                S.mm(bB.t[0:L, hh, 0:L], [(qn.t[:, hh, cs], kn.t[:, hh, cs])], [kn.k(hh), qn.k(hh)], [bB.k()])
            yield
            CP("act", v3(G_X0), bA.t[0:L, h0:h1, 0:L], [bA.k()], KK(G_X0))
            CP("dve", v3(G_P), bB.t[0:L, h0:h1, 0:L], [bB.k()], KK(G_P))
            bC, _ = pbank()
            for hh in HS:
                S.mm(bC.t[0:L, hh, 0:L], [(R["gam"].t[0:4, cs], onehot4.t[:, hh, 0:L]),
                                          (onehot4.t[:, hh, 0:L], R["ngam"].t[0:4, cs]),
                                          (If, negmask.t[0:L, 0:L])],
                     [RK("gam"), RK("ngam"), onehot4.k(), ident_f.k(), negmask.k()], [bC.k()])
            yield
            ACT(v3(G_D), bC.t[0:L, h0:h1, 0:L], AF.Exp, [bC.k()], KK(G_D))
            yield
            TT("dve", v3(G_P), v3(G_P), v3(G_D), ALU.mult, KK(G_P) + KK(G_D), KK(G_P))
            TT("dve", v3(G_Dn), v3(G_D), bc_colL(3), ALU.mult, KK(G_D) + [ck_], KK(G_Dn))
            TT("dve", v3(G_Dn), v3(G_Dn), bc_m(smask), ALU.mult, KK(G_Dn) + k0(smask), KK(G_Dn))
            TT("dve", v3(G_X0), v3(G_X0), v3(G_Dn), ALU.mult, KK(G_X0) + KK(G_Dn), KK(G_X0))
            yield
            bT1, vT1 = pbank()
            for hh in HS:
                TR(vT1[0:L, hh, 0:L], G_X0.t[0:L, hh, 0:L], Ib, KK(G_X0) + [ident_b.k()], [bT1.k()])
            bT2, vT2 = pbank()
            for hh in HS:
                TR(vT2[0:L, hh, 0:L], G_P.t[0:L, hh, 0:L], Ib, KK(G_P) + [ident_b.k()], [bT2.k()])
            nmerge = 3 if L == 128 else 2
            TT("dve", v3(G_X[0]), v3(G_X0), bc_m(bmask[0]), ALU.mult, KK(G_X0) + k0(bmask[0]), KK(G_X[0]))
            for m in range(nmerge):
                TT("pool", v3(G_Xo[m]), v3(G_X0), bc_m(bmask[m + 1]), ALU.mult, KK(G_X0) + k0(bmask[m + 1]), KK(G_Xo[m]))
            yield
            TT("dve", v3(G_Y[0]), vT1[0:L, h0:h1, 0:L], bc_m(bmask[0]), ALU.mult, [bT1.k()] + k0(bmask[0]), KK(G_Y[0]))
            CP("act", v3(G_PT), vT2[0:L, h0:h1, 0:L], [bT2.k()], KK(G_PT))
            yield
            TT("dve", v3(G_Q[0]), v3(G_Y[0]), Ib.unsqueeze(1).to_broadcast([L, nh, L]), ALU.add,
               KK(G_Y[0]) + [ident_b.k()], KK(G_Q[0]))
            yield
            cur, qc = 0, 0
            X, Y = G_X, G_Y
            for k in range(1, 4):
                nx = 1 - cur
                bX, _ = pbank()
                for hh in HS:
                    S.mm(bX.t[0:L, hh, 0:L], [(Y[cur].t[0:L, hh, 0:L], X[cur].t[0:L, hh, 0:L])],
                         KK(X[cur]) + KK(Y[cur]), [bX.k()])
                if k < 3:
                    bY, _ = pbank()
                    for hh in HS:
                        S.mm(bY.t[0:L, hh, 0:L], [(X[cur].t[0:L, hh, 0:L], Y[cur].t[0:L, hh, 0:L])],
                             KK(X[cur]) + KK(Y[cur]), [bY.k()])
                yield
                CP("act", v3(X[nx]), bX.t[0:L, h0:h1, 0:L], [bX.k()], KK(X[nx]))
                if k < 3:
                    CP("dve", v3(Y[nx]), bY.t[0:L, h0:h1, 0:L], [bY.k()], KK(Y[nx]))
                yield
                bQ, _ = pbank()
                for hh in HS:
                    S.mm(bQ.t[0:L, hh, 0:L], [(X[nx].t[0:L, hh, 0:L], G_Q[qc].t[0:L, hh, 0:L])],
                         KK(G_Q[qc]) + KK(X[nx]), [bQ.k()])
                yield
                TT("dve", v3(G_Q[1 - qc]), bQ.t[0:L, h0:h1, 0:L], v3(G_Q[qc]), ALU.add, [bQ.k()] + KK(G_Q[qc]), KK(G_Q[1 - qc]))
                qc = 1 - qc
                cur = nx
                yield
            for m in range(nmerge):
                bT, vT = pbank()
                for hh in HS:
                    TR(vT[0:L, hh, 0:L], G_Q[qc].t[0:L, hh, 0:L], Ib, KK(G_Q[qc]) + [ident_b.k()], [bT.k()])
                bV, _ = pbank()
                for hh in HS:
                    S.mm(bV.t[0:L, hh, 0:L], [(G_Xo[m].t[0:L, hh, 0:L], G_Q[qc].t[0:L, hh, 0:L])],
                         KK(G_Xo[m]) + KK(G_Q[qc]), [bV.k()])
                yield
                CP("act", v3(G_T), vT[0:L, h0:h1, 0:L], [bT.k()], KK(G_T))
                CP("dve", v3(G_V), bV.t[0:L, h0:h1, 0:L], [bV.k()], KK(G_V))
                yield
                bQ, _ = pbank()
                for hh in HS:
                    S.mm(bQ.t[0:L, hh, 0:L], [(G_T.t[0:L, hh, 0:L], G_V.t[0:L, hh, 0:L])],
                         KK(G_T) + KK(G_V), [bQ.k()])
                yield
                TT("dve", v3(G_Q[1 - qc]), bQ.t[0:L, h0:h1, 0:L], v3(G_Q[qc]), ALU.add, [bQ.k()] + KK(G_Q[qc]), KK(G_Q[1 - qc]))
                qc = 1 - qc
                yield
            Qf = G_Q[qc]
            bK, vK = pbank()
            for hh in HS:
                TR(vK[0:L, hh, 0:128], kn.t[:, hh, cs], ident_b.t[:, :], [kn.k(hh), ident_b.k()], [bK.k()])
            bVv, vVv = pbank()
            for hh in HS:
                TR(vVv[0:L, hh, 0:128], qkvs.t[:, 8 + hh, cs], ident_b.t[:, :], [qkvs.k(8 + hh), ident_b.k()], [bVv.k()])
            yield
            TT("dve", G_kw.t[0:L, h0:h1, :], vK[0:L, h0:h1, 0:128], bc_col(0), ALU.mult, [bK.k(), ck_], KK(G_kw))
            TT("dve", G_kh.t[0:L, h0:h1, :], vK[0:L, h0:h1, 0:128], bc_col(1), ALU.mult, [bK.k(), ck_], KK(G_kh))
            TT("dve", G_vb.t[0:L, h0:h1, :], vVv[0:L, h0:h1, 0:128], bc_col(2), ALU.mult, [bVv.k(), ck_], KK(G_vb))
            yield
            bW, _ = pbank()
            for hh in HS:
                S.mm(bW.t[:, hh, 0:L], [(G_kw.t[0:L, hh, :], Qf.t[0:L, hh, 0:L])], KK(G_kw) + KK(Qf), [bW.k()])
            yield
            ACT(G_nw.t[:, h0:h1, 0:L], bW.t[:, h0:h1, 0:L], AF.Copy, [bW.k()], KK(G_nw), scale=-1.0)
            yield
            bVn, _ = pbank()
            for hh in HS:
                S.mm(bVn.t[0:L, hh, :], [(Qf.t[0:L, hh, 0:L], G_vb.t[0:L, hh, :]), (G_nw.t[:, hh, 0:L], S_b.t[:, l, hh, :])],
                     KK(Qf) + KK(G_vb) + KK(G_nw) + [SKs[hh]], [bVn.k()])
            yield
            CP("dve", G_vn.t[0:L, h0:h1, :], bVn.t[0:L, h0:h1, :], [bVn.k()], KK(G_vn))
            yield
            bO, _ = pbank()
            for hh in HS:
                S.mm(bO.t[0:L, hh, :], [(qe.t[:, hh, cs], S_b.t[:, l, hh, :]), (G_PT.t[0:L, hh, 0:L], G_vn.t[0:L, hh, :])],
                     [qe.k(hh), SKs[hh]] + KK(G_PT) + KK(G_vn), [bO.k()])
            bS, _ = pbank()
            for hh in HS:
                S.mm(bS.t[:, hh, :], [(G_kh.t[0:L, hh, :], G_vn.t[0:L, hh, :])], KK(G_kh) + KK(G_vn), [bS.k()])
            TT("dve", sf, sf, egl_c.t[:, h0:h1, b:b + 1].to_broadcast([128, nh, 128]), ALU.mult,
               sfks + egl_c.ks(HS), sfks)
            yield
            ACT(G_T.t[0:L, h0:h1, :], bO.t[0:L, h0:h1, :], AF.Square, [bO.k()], KK(G_T))
            TT("dve", sf, sf, bS.t[:, h0:h1, :], ALU.add, sfks + [bS.k()], sfks)
            yield
            CP("act", sbv, sf, sfks, sks)
            sc0 = ci * 8
            S.op("dve", lambda e: e.reduce_sum(out=G_sc.t[0:L, sc0:sc0 + nh], in_=G_T.t[0:L, h0:h1, :], axis=mybir.AxisListType.X),
                 KK(G_T), KK(G_sc))
            yield
            RSQ(G_sc.t[0:L, sc0 + 4:sc0 + 4 + nh], G_sc.k(ci), G_sc.t[0:L, sc0:sc0 + nh], KK(G_sc), scale=1.0 / DK, bias=EPS)
            yield
            TT("dve", G_on.t[0:L, h0:h1, :], bO.t[0:L, h0:h1, :],
               G_sc.t[0:L, sc0 + 4:sc0 + 4 + nh].unsqueeze(2).to_broadcast([L, nh, 128]), ALU.mult,
               [bO.k()] + KK(G_sc), KK(G_on))
            yield
            bF, vF = pbank()
            for hh in HS:
                TR(vF[:, hh, 0:L], G_on.t[0:L, hh, :], Ib, KK(G_on) + [ident_b.k()], [bF.k()])
            yield
            STT("dve", ymix.t[:, 4 + h0:4 + h1, cs], vF[:, h0:h1, 0:L], pcols.t[:, C_GNG + l:C_GNG + l + 1], zg.t[:, h0:h1, cs],
                ALU.mult, ALU.mult, [bF.k(), pk] + zg.ks(HS), ymix.ks(range(4 + h0, 4 + h1)))

        def gdn_blocks(blocks):
            for b in blocks:
                if sample:
                    for hh in range(H):
                        S.dma(S_f.t[:, l, hh, :], st_s[l, b, hh], (), [S_f.k((l, hh))])
                        CP("act", S_b.t[:, l, hh, :], S_f.t[:, l, hh, :], [S_f.k((l, hh))], [S_b.k((l, hh))])
                gens = [gdn_chain(b, 0, 0, 2, [ph[0], ph[1]]), gdn_chain(b, 1, 2, 4, [ph[2], ph[3]])]
                while gens:
                    for g in list(gens):
                        try:
                            next(g)
                        except StopIteration:
                            gens.remove(g)
                    yield
                if sample:
                    for hh in range(H):
                        S.dma(ns_s[l, b, hh], S_f.t[:, l, hh, :], [S_f.k((l, hh))], ())

        def post(c0, c1, hf):
            segs = [(sidx, max(s0, c0), min(s0 + n, c1) - max(s0, c0)) for (sidx, s0, n) in SEG if s0 < c1 and s0 + n > c0]
            YK = ymix.ks(range(KC))
            for j in range(2):
                slot = w_get(P["out"][hf][j])
                for m in range(4):
                    oc = j * 4 + m
                    p = pm_ring.next()
                    yield from S.mm_gen(p.t[:, c0:c1], [(slot.t[:, kc, m * 128:(m + 1) * 128], ymix.t[:, kc, c0:c1]) for kc in range(KC)],
                                        wk(slot) + YK, [p.k()])
                    for sidx, s0, n in segs:
                        STT("dve", xres.t[:, oc, s0:s0 + n], p.t[:, s0:s0 + n], mcol.t[:, l, 2, oc, sidx:sidx + 1], xres.t[:, oc, s0:s0 + n],
                            ALU.mult, ALU.add, [p.k(), mcol.k(), xres.k(oc)], [xres.k(oc)])
                    yield
            norm_mod(c0, c1, l, 1, segs)
            yield
            for j in range(11):
                slot = w_get(P["gu"][hf][j])
                for m2 in range(2):
                    fc = 2 * j + m2
                    pgt = pm_ring.next()
                    yield from S.mm_gen(pgt.t[:, c0:c1], [(slot.t[:, kc, m2 * 128:(m2 + 1) * 128], hbuf.t[:, kc, c0:c1]) for kc in range(KC)],
                                        wk(slot) + HK, [pgt.k()])
                    yield
                    put = pm_ring.next()
                    yield from S.mm_gen(put.t[:, c0:c1], [(slot.t[:, kc, 256 + m2 * 128:256 + (m2 + 1) * 128], hbuf.t[:, kc, c0:c1]) for kc in range(KC)],
                                        wk(slot) + HK, [put.k()])
                    tf = tmpf_ring.next()
                    ACT(tf.t[:, c0:c1], pgt.t[:, c0:c1], AF.Tanh, [pgt.k()], [tf.k()], scale=0.5)
                    STT("dve", tf.t[:, c0:c1], tf.t[:, c0:c1], 1.0, pgt.t[:, c0:c1], ALU.add, ALU.mult, [tf.k(), pgt.k()], [tf.k()])
                    STT("dve", ffa.t[:, fc, c0:c1], tf.t[:, c0:c1], 0.5, put.t[:, c0:c1], ALU.mult, ALU.mult, [tf.k(), put.k()], [ffa.k()])
                    yield
            for oc in range(KC):
                slot = w_get(P["dn"][hf][oc])
                sv = slot.t[:].rearrange("p a b -> p (a b)")[:, 0:FC * 128].rearrange("p (k c) -> p k c", c=128)
                p = pm_ring.next()
                yield from S.mm_gen(p.t[:, c0:c1], [(sv[:, k, :], ffa.t[:, k, c0:c1]) for k in range(FC)], wk(slot) + [ffa.k()], [p.k()])
                for sidx, s0, n in segs:
                    STT("dve", xres.t[:, oc, s0:s0 + n], p.t[:, s0:s0 + n], mcol.t[:, l, 5, oc, sidx:sidx + 1], xres.t[:, oc, s0:s0 + n],
                        ALU.mult, ALU.add, [p.k(), mcol.k(), xres.k(oc)], [xres.k(oc)])
                yield

        hN, hB = N // 2, nblk // 2
        for _ in gdn_blocks(range(0, hB)):
            pass
        ck(6)
        run_gens([gdn_blocks(range(hB, nblk)), post(0, hN, 0)])
        ck(7)
        for _ in post(hN, N, 1):
            pass
        ck(8)

    def init_stream(stream):
        if stream == 0:
            for hb in (hist_a, hist_b, hist_q):
                MSET("pool", hb.t[:], 0.0, k0(hb))
            for l in range(DEPTH):
                for hh in range(H):
                    MSET("pool", S_f.t[:, l, hh, :], 0.0, [S_f.k((l, hh))])
                    MSET("pool", S_b.t[:, l, hh, :], 0.0, [S_b.k((l, hh))])
            return
        for l in range(DEPTH):
            for s_ in range(ns):
                load_rows_T(hist_a.t[:, l, s_, :, :].rearrange("p c t -> p t c"),
                            st_a[l, s_].rearrange("t (c p) -> (t c) p", p=128), 4, hist_a.k())
                load_rows_T(hist_b.t[:, l, s_, :, :].rearrange("p c t -> p t c"),
                            st_b[l, s_].rearrange("t (c p) -> (t c) p", p=128), 60, hist_b.k())
                load_rows_T(hist_q.t[:, l, s_, :, :].rearrange("p c t -> p t c"),
                            st_q[l, s_].rearrange("t (c p) -> (t c) p", p=128), 36, hist_q.k())

    def finish_stream(stream):
        for l in range(DEPTH):
            slots = [(0, na_p[l], nb_p[l], nq_p[l])] if stream == 0 else \
                [(s_, na_s[l, s_], nb_s[l, s_], nq_s[l, s_]) for s_ in range(ns)]
            for si, da, db, dq in slots:
                store_rows_T(da.rearrange("t (c p) -> (t c) p", p=128), tail_a.t[:, l, si, :, :].rearrange("p t c -> p (t c)"), 4, tail_a.k())
                store_rows_T(db.rearrange("t (c p) -> (t c) p", p=128), tail_b.t[:, l, si, :, :].rearrange("p t c -> p (t c)"), 60, tail_b.k())
                store_rows_T(dq.rearrange("t (c p) -> (t c) p", p=128), tail_q.t[:, l, si, :, :].rearrange("p t c -> p (t c)"), 36, tail_q.k())
            if stream == 0:
                for hh in range(H):
                    S.dma(ns_p[l][hh], S_f.t[:, l, hh, :], [S_f.k((l, hh))], ())

    def _main_body():
        for ti, (stream, tok0, N, first, last) in enumerate(tiles):
            if first:
                init_stream(stream)
            x_src = xp[tok0:tok0 + N, :] if stream == 0 else xs.rearrange("s t d -> (s t) d")[0:N, :]
            load_x(x_src, N)
            for l in range(DEPTH):
                layer(ti, l, stream, N, first, last)
            pb = rms_row(0, N, lambda c: xres.k(c), lambda c: xres.t[:, c, 0:N], KC, mean1024, EPS, "f")
            for c in range(KC):
                tf = tmpf_ring.next()
                TT("dve", tf.t[:, 0:N], xres.t[:, c, 0:N], pb.t[:, 0:N], ALU.mult, [xres.k(c), pb.k()], [tf.k()])
                ACT(xres.t[:, c, 0:N], tf.t[:, 0:N], AF.Copy, [tf.k(), pk], [xres.k(c)], scale=pcols.t[:, C_FNG + c:C_FNG + c + 1])
            y_dst = y_p[tok0:tok0 + N, :] if stream == 0 else y_s.rearrange("s t d -> (s t) d")[0:N, :]
            store_y(y_dst, N)
            if last:
                finish_stream(stream)

    try:
        _main_body()
    except _Stop:
        pass
    S.finish()
    nc._n_ins = S.n_ins
    nc._cnt = dict(S.cnt)
    nc._dcnt = {q: list(v) for q, v in S.dcnt.items()}


W_NAMES = ["norm1_g", "w_ada", "b_ada", "w_in", "conv_a_w", "conv_b_w", "conv_b_b", "ln_b_g", "ln_b_b",
           "conv_qkv_w", "a_log", "dt_bias", "gdn_norm_g", "w_out", "norm2_g", "w_gate_up", "w_down"]


def make_in_maps(inp, tp=TP, n_cores=8):
    f = lambda a: np.ascontiguousarray(np.asarray(a, dtype=np.float32))
    shared = {n: f(inp[n]) for n in W_NAMES}
    shared["final_norm_g"] = f(inp["final_norm_g"]).reshape(1, D)
    maps = []
    for i in range(n_cores):
        m = dict(shared)
        m["xp"] = f(inp["x_prompt"][i, :tp])
        sl = slice(NS * i, NS * (i + 1))
        m["xs"] = f(inp["x_sample"][sl])
        m["st_a"] = f(inp["state_conv_a"][:, sl])
        m["st_b"] = f(inp["state_conv_b"][:, sl])
        m["st_q"] = f(inp["state_conv_qkv"][:, sl])
        m["st_s"] = f(inp["state_gdn"][:, sl])
        m["c_all"] = f(np.concatenate([inp["c_prompt"][i:i + 1], inp["c_sample"][sl]], axis=0))
        maps.append(m)
    return maps


_NC_CACHE = {}


def _get_nc(tp=TP):
    if tp not in _NC_CACHE:
        _NC_CACHE[tp] = build(tp=tp)
    return _NC_CACHE[tp]


def assemble(results, tp=TP):
    n = len(results)
    g = lambda k: [np.asarray(r[k], dtype=np.float32) for r in results]
    y_p = np.stack(g("y_p"), 0)
    y_s = np.concatenate(g("y_s"), 0)
    pa = np.stack(g("na_p"), 1)
    pb = np.stack(g("nb_p"), 1)
    pq = np.stack(g("nq_p"), 1)
    ps_ = np.stack(g("ns_p"), 1)
    sa = np.concatenate(g("na_s"), 1)
    sb_ = np.concatenate(g("nb_s"), 1)
    sq = np.concatenate(g("nq_s"), 1)
    ss = np.concatenate(g("ns_s"), 1)
    return (y_p, y_s, pa, pb, pq, ps_, sa, sb_, sq, ss)


def kernel(**inputs):
    nc = _get_nc(TP)
    in_maps = make_in_maps(inputs, TP, 8)
    res = run_bass_kernel_spmd(nc, in_maps, core_ids=list(range(8)))
    return assemble(res.results, TP)
```
